# Optimizing a Trainium2 kernel written in Bass

```python
import math
import jax, jax.numpy as jnp
from jax import lax
import numpy as np

D_MODEL = 2048
BATCH = 1
SEQ = 8192
DEPTH = 4

GRID_W = 64
CTX_LEN = 256
EPS = 1e-6
ROPE_BASE = 10000.0
N_BRANCH = 4
BRANCH_W = 512
A_HEADS = 8
A_KV_HEADS = 2
A_HEAD_DIM = 64
WINDOW = 128
A_BLOCK = 128
B_HEADS = 4
B_NOPE = 128
B_ROPE = 64
B_VDIM = 128
B_Q_LORA = 384
B_KV_LORA = 128
B_BLOCK = 128
C_HEADS = 4
C_DK = 128
C_DV = 128
C_CHUNK = 16
POOL_WINDOWS = (2, 4, 8, 16)
POOL_GROUP = BRANCH_W // len(POOL_WINDOWS)

A_K0 = 0
A_V0 = A_K0 + A_KV_HEADS * A_HEAD_DIM
B_CKV0 = A_V0 + A_KV_HEADS * A_HEAD_DIM
B_KR0 = B_CKV0 + B_KV_LORA
C_FF0 = B_KR0 + B_ROPE
C_FB0 = C_FF0 + C_HEADS * C_DK
C_I0 = C_FB0 + C_HEADS * C_DK
N_KV_COLS = C_I0 + C_HEADS * C_DV
A_Q0 = N_KV_COLS
B_CQ0 = A_Q0 + A_HEADS * A_HEAD_DIM
C_Q0 = B_CQ0 + B_Q_LORA
D_X0 = C_Q0 + C_HEADS * C_DK
GATE0 = D_X0 + BRANCH_W
MERGE0 = GATE0 + N_BRANCH * BRANCH_W
N_IN_COLS = MERGE0 + N_BRANCH * D_MODEL

kernel_name = "hybrid_parallel_mixer_dit_trunk"


def rms_norm(x, g):
    xf = x.astype(jnp.float32)
    y = xf * lax.rsqrt(jnp.mean(xf * xf, axis=-1, keepdims=True) + EPS)
    return (y * g.astype(jnp.float32)).astype(x.dtype)


def rope_1d(x, pos):
    half = x.shape[-1] // 2
    freqs = ROPE_BASE ** (-jnp.arange(half, dtype=jnp.float32) / half)
    ang = pos.astype(jnp.float32)[:, None] * freqs[None, :]
    cos = jnp.cos(ang)[None, :, None, :]
    sin = jnp.sin(ang)[None, :, None, :]
    xf = x.astype(jnp.float32)
    x1, x2 = xf[..., :half], xf[..., half:]
    return jnp.concatenate([x1 * cos - x2 * sin, x1 * sin + x2 * cos], axis=-1).astype(x.dtype)


def rope_2d(x, rows, cols):
    d = x.shape[-1] // 2
    return jnp.concatenate([rope_1d(x[..., :d], rows), rope_1d(x[..., d:], cols)], axis=-1)


def window_gqa(q, k, v, kc, vc, sink):
    B, T = q.shape[:2]
    N = T // A_BLOCK
    G = A_HEADS // A_KV_HEADS
    qb = q.reshape(B, N, A_BLOCK, A_KV_HEADS, G, A_HEAD_DIM)
    pad = ((0, 0), (A_BLOCK, A_BLOCK), (0, 0), (0, 0))
    kp = jnp.pad(k, pad).reshape(B, N + 2, A_BLOCK, A_KV_HEADS, A_HEAD_DIM)
    vp = jnp.pad(v, pad).reshape(B, N + 2, A_BLOCK, A_KV_HEADS, A_HEAD_DIM)
    kb = jnp.concatenate([kp[:, :-2], kp[:, 1:-1], kp[:, 2:]], axis=2)
    vb = jnp.concatenate([vp[:, :-2], vp[:, 1:-1], vp[:, 2:]], axis=2)
    scale = A_HEAD_DIM ** -0.5
    s_loc = jnp.einsum('bnqhgd,bnkhd->bnhgqk', qb, kb, preferred_element_type=jnp.float32) * scale
    qi = jnp.arange(A_BLOCK)[:, None]
    kj = jnp.arange(3 * A_BLOCK)[None, :] - A_BLOCK
    in_win = jnp.abs(kj - qi) <= WINDOW
    kpos = jnp.arange(N)[:, None] * A_BLOCK + kj
    valid = (kpos >= 0) & (kpos < T)
    mask = in_win[None] & valid[:, None, :]
    s_loc = jnp.where(mask[None, :, None, None], s_loc, -jnp.inf)
    s_ctx = jnp.einsum('bnqhgd,bchd->bnhgqc', qb, kc, preferred_element_type=jnp.float32) * scale
    s_sink = jnp.broadcast_to(sink.astype(jnp.float32).reshape(A_KV_HEADS, G)[None, None, :, :, None, None],
                              s_loc.shape[:-1] + (1,))
    p = jax.nn.softmax(jnp.concatenate([s_loc, s_ctx, s_sink], axis=-1), axis=-1).astype(v.dtype)
    L = kc.shape[1]
    o = (jnp.einsum('bnhgqk,bnkhd->bnqhgd', p[..., :3 * A_BLOCK], vb)
         + jnp.einsum('bnhgqc,bchd->bnqhgd', p[..., 3 * A_BLOCK:3 * A_BLOCK + L], vc))
    return o.reshape(B, T, A_HEADS * A_HEAD_DIM)


def ctx_gqa(qc, kc, vc, sink):
    B, L = qc.shape[:2]
    G = A_HEADS // A_KV_HEADS
    qg = qc.reshape(B, L, A_KV_HEADS, G, A_HEAD_DIM)
    s = jnp.einsum('blhgd,bmhd->bhglm', qg, kc, preferred_element_type=jnp.float32) * A_HEAD_DIM ** -0.5
    s_sink = jnp.broadcast_to(sink.astype(jnp.float32).reshape(A_KV_HEADS, G)[None, :, :, None, None],
                              s.shape[:-1] + (1,))
    p = jax.nn.softmax(jnp.concatenate([s, s_sink], axis=-1), axis=-1).astype(vc.dtype)
    o = jnp.einsum('bhglm,bmhd->blhgd', p[..., :L], vc)
    return o.reshape(B, L, A_HEADS * A_HEAD_DIM)


def branch_window_attn(p, pc, rows, cols, sink, need_ctx):
    B, T = p.shape[:2]
    L = pc.shape[1]
    q = rope_2d(p[..., A_Q0:A_Q0 + A_HEADS * A_HEAD_DIM].reshape(B, T, A_HEADS, A_HEAD_DIM), rows, cols)
    k = rope_2d(p[..., A_K0:A_V0].reshape(B, T, A_KV_HEADS, A_HEAD_DIM), rows, cols)
    v = p[..., A_V0:B_CKV0].reshape(B, T, A_KV_HEADS, A_HEAD_DIM)
    kc = pc[..., A_K0:A_V0].reshape(B, L, A_KV_HEADS, A_HEAD_DIM)
    vc = pc[..., A_V0:B_CKV0].reshape(B, L, A_KV_HEADS, A_HEAD_DIM)
    o = window_gqa(q, k, v, kc, vc, sink)
    oc = None
    if need_ctx:
        qc = pc[..., A_Q0:A_Q0 + A_HEADS * A_HEAD_DIM].reshape(B, L, A_HEADS, A_HEAD_DIM)
        oc = ctx_gqa(qc, kc, vc, sink)
    return o, oc


def mla_kv(pp, kv_norm, w_ukv):
    B, T = pp.shape[:2]
    ckv = rms_norm(pp[..., B_CKV0:B_CKV0 + B_KV_LORA], kv_norm)
    kv = (ckv @ w_ukv).reshape(B, T, B_HEADS, B_NOPE + B_VDIM)
    k_rope = pp[..., B_KR0:B_KR0 + B_ROPE][:, :, None, :]
    return kv[..., :B_NOPE], kv[..., B_NOPE:], k_rope


def mla_q(pp, q_norm, w_uq):
    B, T = pp.shape[:2]
    cq = rms_norm(pp[..., B_CQ0:B_CQ0 + B_Q_LORA], q_norm)
    q = (cq @ w_uq).reshape(B, T, B_HEADS, B_NOPE + B_ROPE)
    return q[..., :B_NOPE], q[..., B_NOPE:]


def mla_keys(k_nope, k_rope):
    return jnp.concatenate([k_nope, jnp.broadcast_to(k_rope, k_nope.shape[:-1] + (B_ROPE,))], axis=-1)


def branch_mla(p, pc, rows, cols, q_norm, w_uq, kv_norm, w_ukv, need_ctx):
    B, T = p.shape[:2]
    scale = (B_NOPE + B_ROPE) ** -0.5
    k_nope, v, k_rope = mla_kv(p, kv_norm, w_ukv)
    k = mla_keys(k_nope, rope_2d(k_rope, rows, cols))
    q_nope, q_rope = mla_q(p, q_norm, w_uq)
    q = jnp.concatenate([q_nope, rope_2d(q_rope, rows, cols)], axis=-1)
    kc_nope, vc, kc_rope = mla_kv(pc, kv_norm, w_ukv)
    kc = mla_keys(kc_nope, kc_rope)
    k_all = jnp.concatenate([kc, k], axis=1)
    v_all = jnp.concatenate([vc, v], axis=1)
    N = T // B_BLOCK
    qb = jnp.moveaxis(q.reshape(B, N, B_BLOCK, B_HEADS, B_NOPE + B_ROPE), 1, 0)

    def block(qi):
        s = jnp.einsum('bqhd,bkhd->bhqk', qi, k_all, preferred_element_type=jnp.float32) * scale
        pr = jax.nn.softmax(s, axis=-1).astype(v_all.dtype)
        return jnp.einsum('bhqk,bkhd->bqhd', pr, v_all)

    o = jnp.moveaxis(lax.map(block, qb), 0, 1).reshape(B, T, B_HEADS * B_VDIM)
    oc = None
    if need_ctx:
        L = pc.shape[1]
        qc_nope, qc_rope = mla_q(pc, q_norm, w_uq)
        qc = jnp.concatenate([qc_nope, qc_rope], axis=-1)
        s = jnp.einsum('blhd,bmhd->bhlm', qc, kc, preferred_element_type=jnp.float32) * scale
        pr = jax.nn.softmax(s, axis=-1).astype(vc.dtype)
        oc = jnp.einsum('bhlm,bmhd->blhd', pr, vc).reshape(B, L, B_HEADS * B_VDIM)
    return o, oc


def hgrn_gates(z, lb):
    zf = z.astype(jnp.float32)
    log_f = jnp.logaddexp(jnp.log(lb), jnp.log1p(-lb) + jax.nn.log_sigmoid(zf))
    k = (1.0 - lb) * jax.nn.sigmoid(-zf)
    return k, log_f


def to_chunks(a):
    B, T = a.shape[:2]
    return a.reshape((B, T // C_CHUNK, C_CHUNK) + a.shape[2:])


def chunk_states(k, v, b, s0):
    b_last = b[:, :, -1]
    u = jnp.einsum('bnchk,bnchv->bnhkv', k * jnp.exp(b_last[:, :, None] - b), v)
    a = jnp.exp(b_last)

    def step(s, av):
        a_n, u_n = av
        return a_n[..., None] * s + u_n, s

    s_fin, s_start = lax.scan(step, s0, (jnp.moveaxis(a, 1, 0), jnp.moveaxis(u, 1, 0)))
    return s_start, s_fin


def gla_final_state(k, v, log_f, s0):
    kc, vc = to_chunks(k), to_chunks(v)
    b = jnp.cumsum(to_chunks(log_f), axis=2)
    return chunk_states(kc, vc, b, s0)[1]


def gla_chunkwise(q, k, v, log_f, s0):
    B, T, H, _ = q.shape
    qc = to_chunks(q * C_DK ** -0.5)
    kc, vc = to_chunks(k), to_chunks(v)
    b = jnp.cumsum(to_chunks(log_f), axis=2)
    s_start, s_fin = chunk_states(kc, vc, b, s0)
    idx = jnp.arange(C_CHUNK)
    lower = (idx[:, None] >= idx[None, :])[None, None, :, :, None, None]
    decay = jnp.exp(jnp.where(lower, b[:, :, :, None] - b[:, :, None, :], -jnp.inf))
    att = jnp.einsum('bnihk,bnijhk,bnjhk->bnhij', qc, decay, kc)
    o = (jnp.einsum('bnhij,bnjhv->bnihv', att, vc)
         + jnp.einsum('bnihk,nbhkv->bnihv', qc * jnp.exp(b), s_start))
    return o.reshape(B, T, H, v.shape[-1]), s_fin


def hgrn_inputs(pp, lb_f, lb_b):
    B, T = pp.shape[:2]
    v = pp[..., C_I0:N_KV_COLS].astype(jnp.float32).reshape(B, T, C_HEADS, C_DV)
    k_f, lf_f = hgrn_gates(pp[..., C_FF0:C_FB0], lb_f)
    k_b, lf_b = hgrn_gates(pp[..., C_FB0:C_I0], lb_b)
    shp = (B, T, C_HEADS, C_DK)
    return v, k_f.reshape(shp), lf_f.reshape(shp), k_b.reshape(shp), lf_b.reshape(shp)


def flip(a):
    return jnp.flip(a, axis=1)


def branch_hgrn(p, pc, lb_f, lb_b, norm_g, need_ctx):
    B, T = p.shape[:2]
    L = pc.shape[1]
    s0 = jnp.zeros((B, C_HEADS, C_DK, C_DV), jnp.float32)
    v, k_f, lf_f, k_b, lf_b = hgrn_inputs(p, lb_f, lb_b)
    q = jax.nn.silu(p[..., C_Q0:D_X0].astype(jnp.float32)).reshape(B, T, C_HEADS, C_DK)
    vc, kc_f, lfc_f, kc_b, lfc_b = hgrn_inputs(pc, lb_f, lb_b)
    oc = None
    if need_ctx:
        qc = jax.nn.silu(pc[..., C_Q0:D_X0].astype(jnp.float32)).reshape(B, L, C_HEADS, C_DK)
        oc_f, sc_f = gla_chunkwise(qc, kc_f, vc, lfc_f, s0)
        oc_b, sc_b = gla_chunkwise(flip(qc), flip(kc_b), flip(vc), flip(lfc_b), s0)
        oc = rms_norm(oc_f + flip(oc_b), norm_g.reshape(C_HEADS, C_DV)).reshape(B, L, C_HEADS * C_DV).astype(p.dtype)
    else:
        sc_f = gla_final_state(kc_f, vc, lfc_f, s0)
        sc_b = gla_final_state(flip(kc_b), flip(vc), flip(lfc_b), s0)
    o_f, _ = gla_chunkwise(q, k_f, v, lf_f, sc_f)
    o_b, _ = gla_chunkwise(flip(q), flip(k_b), flip(v), flip(lf_b), sc_b)
    o = rms_norm(o_f + flip(o_b), norm_g.reshape(C_HEADS, C_DV)).reshape(B, T, C_HEADS * C_DV).astype(p.dtype)
    return o, oc


def multiscale_pool(u, w_pool, pool_scale):
    B, T, W = u.shape
    uf = u.astype(jnp.float32)
    csum = jnp.concatenate([jnp.zeros((B, 1, W), jnp.float32), jnp.cumsum(uf, axis=1)], axis=1)
    t = jnp.arange(T)
    outs = []
    for g, w in enumerate(POOL_WINDOWS):
        lo = jnp.clip(t - w // 2, 0, T)
        hi = jnp.clip(t + w - w // 2, 0, T)
        sl = slice(g * POOL_GROUP, (g + 1) * POOL_GROUP)
        mean = (csum[:, hi][..., sl] - csum[:, lo][..., sl]) / (hi - lo).astype(jnp.float32)[None, :, None]
        d = (mean - uf[..., sl]).astype(u.dtype)
        outs.append(d @ w_pool[g])
    return jnp.concatenate(outs, axis=-1) * pool_scale


def branch_pool(p, pc, w_pool, pool_scale, need_ctx):
    o = multiscale_pool(p[..., D_X0:GATE0], w_pool, pool_scale)
    oc = multiscale_pool(pc[..., D_X0:GATE0], w_pool, pool_scale) if need_ctx else None
    return o, oc


def merge(pp, outs, w_branch, w_out):
    B, T = pp.shape[:2]
    gates = pp[..., GATE0:MERGE0].reshape(B, T, N_BRANCH, BRANCH_W)
    mg = pp[..., MERGE0:N_IN_COLS].reshape(B, T, N_BRANCH, D_MODEL)
    ys = jnp.stack(outs, axis=2) * jax.nn.silu(gates)
    yb = jnp.einsum('btnw,nwd->btnd', ys, w_branch)
    merged = jnp.sum(jax.nn.sigmoid(mg) * yb, axis=2)
    return merged @ w_out


def setup_inputs(seed: int = 0) -> dict:
    key = jax.random.key(seed)
    ks = jax.random.split(key, 20)
    f32 = jnp.float32

    def nrm(k, shape, s):
        return jax.random.normal(k, shape, f32) * s

    return {
        "x": nrm(ks[0], (BATCH, SEQ, D_MODEL), 1.0),
        "c": nrm(ks[1], (BATCH, D_MODEL), 1.0),
        "ctx": nrm(ks[2], (BATCH, CTX_LEN, D_MODEL), 1.0),
        "c_ctx": nrm(ks[3], (D_MODEL,), 1.0),
        "w_ada": nrm(ks[4], (DEPTH, D_MODEL, 3 * D_MODEL), 0.5 * D_MODEL ** -0.5),
        "b_ada": nrm(ks[5], (DEPTH, 3 * D_MODEL), 0.02),
        "g_pre": 1.0 + nrm(ks[6], (DEPTH, D_MODEL), 0.05),
        "g_post": 1.0 + nrm(ks[7], (DEPTH, D_MODEL), 0.05),
        "w_in": nrm(ks[8], (DEPTH, D_MODEL, N_IN_COLS), D_MODEL ** -0.5),
        "a_sink": nrm(ks[9], (DEPTH, A_HEADS), 0.5),
        "mla_q_norm": 1.0 + nrm(ks[10], (DEPTH, B_Q_LORA), 0.05),
        "w_uq": nrm(ks[11], (DEPTH, B_Q_LORA, B_HEADS * (B_NOPE + B_ROPE)), B_Q_LORA ** -0.5),
        "mla_kv_norm": 1.0 + nrm(ks[12], (DEPTH, B_KV_LORA), 0.05),
        "w_ukv": nrm(ks[13], (DEPTH, B_KV_LORA, B_HEADS * (B_NOPE + B_VDIM)), B_KV_LORA ** -0.5),
        "hgrn_lb": nrm(ks[14], (DEPTH, 2, C_HEADS * C_DK), 1.0),
        "hgrn_norm": 1.0 + nrm(ks[15], (DEPTH, C_HEADS * C_DV), 0.05),
        "w_pool": nrm(ks[16], (DEPTH, len(POOL_WINDOWS), POOL_GROUP, POOL_GROUP), POOL_GROUP ** -0.5),
        "pool_scale": 1.0 + nrm(ks[17], (DEPTH, BRANCH_W), 0.1),
        "w_branch": nrm(ks[18], (DEPTH, N_BRANCH, BRANCH_W, D_MODEL), BRANCH_W ** -0.5),
        "w_out": nrm(ks[19], (DEPTH, D_MODEL, D_MODEL), D_MODEL ** -0.5),
    }


def reference(x, c, ctx, c_ctx, w_ada, b_ada, g_pre, g_post, w_in, a_sink, mla_q_norm, w_uq,
              mla_kv_norm, w_ukv, hgrn_lb, hgrn_norm, w_pool, pool_scale, w_branch, w_out):
    T = x.shape[1]
    ROWS = T // GRID_W
    rows = jnp.repeat(jnp.arange(ROWS, dtype=jnp.int32), GRID_W)
    cols = jnp.tile(jnp.arange(GRID_W, dtype=jnp.int32), ROWS)
    lb_all = jnp.cumsum(jax.nn.softmax(hgrn_lb.astype(jnp.float32), axis=0), axis=0)
    lb_all = lb_all - lb_all[:1]
    xc = ctx
    for li in range(DEPTH):
        need_ctx = li < DEPTH - 1
        shift, scale, gate = jnp.split(jax.nn.silu(c) @ w_ada[li] + b_ada[li], 3, axis=-1)
        shift_c, scale_c, gate_c = jnp.split(jax.nn.silu(c_ctx) @ w_ada[li] + b_ada[li], 3, axis=-1)
        h = rms_norm(x, g_pre[li]) * (1.0 + scale[:, None]) + shift[:, None]
        hc = rms_norm(xc, g_pre[li]) * (1.0 + scale_c) + shift_c
        p = h @ w_in[li]
        pc = hc @ (w_in[li] if need_ctx else w_in[li][:, :N_KV_COLS])
        oa, oa_c = branch_window_attn(p, pc, rows, cols, a_sink[li], need_ctx)
        ob, ob_c = branch_mla(p, pc, rows, cols, mla_q_norm[li], w_uq[li], mla_kv_norm[li], w_ukv[li], need_ctx)
        oh, oh_c = branch_hgrn(p, pc, lb_all[li, 0], lb_all[li, 1], hgrn_norm[li], need_ctx)
        od, od_c = branch_pool(p, pc, w_pool[li], pool_scale[li], need_ctx)
        y = merge(p, (oa, ob, oh, od), w_branch[li], w_out[li])
        x = x + gate[:, None] * rms_norm(y, g_post[li])
        if need_ctx:
            yc = merge(pc, (oa_c, ob_c, oh_c, od_c), w_branch[li], w_out[li])
            xc = xc + gate_c * rms_norm(yc, g_post[li])
    return x
```

```python
import math
from contextlib import ExitStack
import numpy as np
import ml_dtypes
import concourse.bass as bass
import concourse.mybir as mybir
from concourse.bass_utils import run_bass_kernel_spmd

F32 = mybir.dt.float32
BF16 = mybir.dt.bfloat16
AF = mybir.ActivationFunctionType
ALU = mybir.AluOpType
AX = mybir.AxisListType

NCORES = 8
D = 2048
KC = 16
DEPTH = 4
OWN = 1024
CTXL = 256
TOK = 1280
NT = 10
EPS = 1e-6
NIN = 14144
MERGE0 = 5952
GATE0 = 3904
RANGES = [(0, 512), (512, 512), (1024, 256)]
CH = 32
NCHK = TOK // CH
WAB = 3072 + 516
WCD = 8 * 129 + 64
UW = OWN + 16
UCW = CTXL + 16
NOCC = False


class Tracker:
    def __init__(self, nc, es):
        self.nc = nc
        self.E = dict(pe=nc.tensor, act=nc.scalar, dve=nc.vector, pool=nc.gpsimd, sp=nc.sync)
        self.csem = {e: es.enter_context(nc.semaphore("c_" + e)) for e in ("pe", "act", "dve", "pool")}
        self.ccnt = {e: 0 for e in self.csem}
        self.waited = {e: {} for e in self.E}
        self.lastw = {}
        self.readers = {}
        self.dring = {}
        self.dptr = {}
        for q, n in (("sp", 8), ("pool", 6), ("act", 2)):
            self.dring[q] = [dict(name=f"d_{q}{i}", sem=es.enter_context(nc.semaphore(f"d_{q}{i}")), val=0)
                             for i in range(n)]
            self.dptr[q] = 0
        self.cc_sem = es.enter_context(nc.semaphore("ccs"))
        self.cc_cnt = 0

    def _wait(self, eng, ev):
        name, sem, val = ev
        if self.waited[eng].get(name, 0) >= val:
            return
        self.E[eng].wait_ge(sem, val)
        self.waited[eng][name] = val

    def _deps(self, eng, reads, writes):
        best = {}

        def add(ev):
            if ev[0] not in best or best[ev[0]][2] < ev[2]:
                best[ev[0]] = ev
        for r in reads:
            if r in self.lastw:
                add(self.lastw[r])
        for w in writes:
            if w in self.lastw:
                add(self.lastw[w])
            for ev in self.readers.get(w, {}).values():
                add(ev)
        for ev in best.values():
            if eng == "pe" and ev[0] == "c_pe":
                continue
            self._wait(eng, ev)

    def _record(self, ev, reads, writes):
        for w in writes:
            self.lastw[w] = ev
            self.readers[w] = {}
        for r in reads:
            if r not in writes:
                d = self.readers.setdefault(r, {})
                if ev[0] not in d or d[ev[0]][2] < ev[2]:
                    d[ev[0]] = ev

    def op(self, eng, fn, reads=(), writes=()):
        self._deps(eng, reads, writes)
        ins = fn(self.E[eng])
        self.ccnt[eng] += 1
        ins.then_inc(self.csem[eng], 1)
        self._record(("c_" + eng, self.csem[eng], self.ccnt[eng]), reads, writes)

    def dma(self, q, out, in_, reads=(), writes=(), slow=False):
        self._deps(q, reads, writes)
        ring = self.dring[q]
        slot = ring[self.dptr[q] % len(ring)]
        self.dptr[q] += 1
        if slot["val"] > 0:
            self._wait(q, (slot["name"], slot["sem"], slot["val"]))
        if slow:
            ins = self.E[q].dma_start(out=out, in_=in_, allow_slow_non_contiguous=True)
        else:
            ins = self.E[q].dma_start(out=out, in_=in_)
        slot["val"] += 16
        ins.then_inc(slot["sem"], 16)
        self._record((slot["name"], slot["sem"], slot["val"]), reads, writes)

    def collective(self, in_ap, out_ap, reads, writes):
        if NOCC:
            rows = in_ap.shape[0]
            for r in range(NCORES):
                self.dma("sp", out_ap[r * rows:(r + 1) * rows, :], in_ap, reads, writes)
            return
        self._deps("pool", reads, writes)
        if self.cc_cnt > 0:
            self._wait("pool", ("cc", self.cc_sem, self.cc_cnt))
        ins = self.nc.gpsimd.collective_compute(
            "AllGather", ALU.bypass, replica_groups=[list(range(NCORES))], ins=[in_ap], outs=[out_ap])
        self.cc_cnt += 1
        ins.then_inc(self.cc_sem, 1)
        self._record(("cc", self.cc_sem, self.cc_cnt), reads, writes)

    def all_events(self):
        evs = [("c_" + e, self.csem[e], self.ccnt[e]) for e in self.csem if self.ccnt[e] > 0]
        for q in self.dring:
            for s in self.dring[q]:
                if s["val"] > 0:
                    evs.append((s["name"], s["sem"], s["val"]))
        if self.cc_cnt > 0:
            evs.append(("cc", self.cc_sem, self.cc_cnt))
        return evs

    def barrier(self, engines=("pe", "act", "dve", "pool", "sp")):
        evs = self.all_events()
        for e in engines:
            for ev in evs:
                if ev[0] == "c_" + e:
                    continue
                self._wait(e, ev)


class Region:
    def __init__(self, ap, nelem, name):
        self.ap = ap
        self.n = nelem
        self.name = name
        self.off = 0
        self.gen = 0

    def reset(self, off=0):
        self.off = off
        self.gen += 1

    def alloc(self, shape, dt, tag):
        per = int(np.prod(shape[1:]))
        nb = per * (4 if dt == F32 else 2)
        nb = (nb + 63) // 64 * 64
        ne = nb // 2
        assert self.off + ne <= self.n, f"region overflow {tag}: {self.off + ne} > {self.n}"
        v = self.ap[0:shape[0], self.off:self.off + (per * (2 if dt == F32 else 1))]
        self.off += ne
        if dt == F32:
            v = v.bitcast(F32)
        if len(shape) == 3:
            v = v.rearrange("p (a b) -> p a b", a=shape[1])
        elif len(shape) == 4:
            v = v.rearrange("p (a b c) -> p a b c", a=shape[1], b=shape[2])
        return v, f"{self.name}{self.gen}_{tag}"


class WStream:
    def __init__(self, T, bufs):
        self.T = T
        self.bufs = bufs
        self.plan = []
        self.issued = 0
        self.consumed = 0

    def add(self, tag, parts_fn):
        self.plan.append((tag, parts_fn))

    def _issue(self):
        i = self.issued
        buf, res = self.bufs[i % len(self.bufs)]
        for dst, src in self.plan[i][1](buf):
            self.T.dma("pool", dst, src, reads=(), writes=(res,))
        self.issued += 1

    def next(self, tag):
        i = self.consumed
        assert self.plan[i][0] == tag, (self.plan[i][0], tag)
        while self.issued < min(len(self.plan), i + len(self.bufs)):
            self._issue()
        self.consumed += 1
        return self.bufs[i % len(self.bufs)]


def build(depth=DEPTH, dbg=False, plan=None, split=False):
    nc = bass.Bass("TRN2", target_bir_lowering=False)
    if plan is None:
        plan = [(sg_, l_) for l_ in range(depth) for sg_ in ("pre", "post")]
    final_li = max(l_ for _, l_ in plan)
    declared = set()

    def din(name, shape, dt=F32):
        declared.add(name)
        return nc.dram_tensor(name, list(shape), dt, kind="ExternalInput").ap()

    class LW:
        def __init__(self, name, shape):
            self.name, self.shape, self.c = name, shape, {}

        def __getitem__(self, li):
            if li not in self.c:
                self.c[li] = din(f"{self.name}_{li}", self.shape)
            return self.c[li]

    def dscr(name, shape, dt=F32):
        return nc.dram_tensor(name, list(shape), dt).ap()

    x_in = din("x", [OWN, D])
    ctx_in = din("ctx", [CTXL, D])
    cc_in = din("cc", [2, D])
    w_ada = LW("w_ada", [D, 3 * D])
    b_ada = LW("b_ada", [1, 3 * D])
    g_pre = LW("g_pre", [1, D])
    g_post = LW("g_post", [1, D])
    w_in = LW("w_in", [D, NIN])
    a_sink = din("a_sink", [1, DEPTH * 8])
    q_norm = LW("mla_q_norm", [1, 384])
    w_uq = LW("w_uq", [384, 768])
    kv_norm = LW("mla_kv_norm", [1, 128])
    w_ukv = LW("w_ukv", [128, 1024])
    w_ukT = LW("w_ukT", [4, 128, 128])
    hgrn_lb = din("hgrn_lb", [DEPTH, 2, 512])
    hgrn_norm = LW("hgrn_norm", [1, 512])
    w_pool = LW("w_pool", [4, 128, 128])
    pool_scale = LW("pool_scale", [1, 512])
    w_branch = LW("w_branch", [4, 512, D])
    w_out = LW("w_out", [D, D])
    ropeC_in = din("ropeC", [128, 8, 64])
    ropeS_in = din("ropeS", [128, 8, 64])
    cmask_in = din("cmask", [128, 6 * 128 + 4])
    cmask2_in = din("cmask2", [128, 128 + TOK])
    sel_in = din("sel", [128, 32])
    icnt_in = din("icnt", [128, 2, 4, 16])

    out = nc.dram_tensor("out", [OWN, D], F32, kind="ExternalOutput").ap() if (final_li == DEPTH - 1 and ("post", final_li) in plan) or not split else None
    if dbg:
        dbg_ys = nc.dram_tensor("dbg_ys", [128, 16, TOK], BF16, kind="ExternalOutput").ap()
        dbg_xs = nc.dram_tensor("dbg_xs", [TOK, D], F32, kind="ExternalOutput").ap()
        dbg_h = nc.dram_tensor("dbg_h", [128, KC, TOK], BF16, kind="ExternalOutput").ap()

    y_d = dscr("y_d", [TOK, D])
    has_pre = any(sg_ == "pre" for sg_, _ in plan)
    has_post = any(sg_ == "post" for sg_, _ in plan)
    SCR = dict(gg_d=([2, 128, D], F32), qTA_d=([NT, 128, 512], BF16), qabs_d=([4, 128, TOK], BF16),
               qrT_d=([128, 2, TOK], BF16), kvctx_d=([128, 2, 3, 128], BF16), u_d=([128, 4, UW], F32),
               uc_d=([128, 4, UCW], F32), Qh_d=([8, 128, OWN], BF16), osum_d=([4, 128, TOK], F32),
               sc_d=([8, 128, 128], F32), mod_d=([128, 64], F32), kv_d=([128, 12 * 128 + 12 * 130], BF16))
    DW, DR = {}, {}
    for nm_, (shp_, dt_) in SCR.items():
        if split:
            if has_pre:
                DW[nm_] = nc.dram_tensor("w_" + nm_, list(shp_), dt_, kind="ExternalOutput").ap()
            if has_post:
                DR[nm_] = din("r_" + nm_, shp_, dt_)
        else:
            DW[nm_] = DR[nm_] = dscr(nm_, shp_, dt_)
    if split:
        xinAB = nc.dram_tensor("w_xinAB", [128, WAB], BF16, kind="ExternalOutput").ap() if has_pre else None
        xinCD = nc.dram_tensor("w_xinCD", [128, WCD], F32, kind="ExternalOutput").ap() if has_pre else None
        xoutAB = din("r_xoutAB", [NCORES * 128, WAB], BF16) if has_post else None
        xoutCD = din("r_xoutCD", [NCORES * 128, WCD]) if has_post else None
        xs_w = nc.dram_tensor("w_xs", [TOK, D], F32, kind="ExternalOutput").ap() if has_post else None
        xs_r = din("r_xs", [TOK, D]) if (has_post and plan[0] == ("post", plan[0][1]) and plan[0][1] > 0) else None
    else:
        xinAB = dscr("xinAB", [128, WAB], BF16)
        xoutAB = dscr("xoutAB", [NCORES * 128, WAB], BF16)
        xinCD = dscr("xinCD", [128, WCD])
        xoutCD = dscr("xoutCD", [NCORES * 128, WCD])
        xs_w = xs_r = dscr("xs_d", [TOK, D])

    es = ExitStack()
    with es:
        def sb(name, shape, dt):
            return es.enter_context(nc.sbuf_tensor("s_" + name, list(shape), dt))

        def pst(name, shape, dt):
            return es.enter_context(nc.psum_tensor(name, list(shape), dt))

        ysT = sb("ysT", [128, 16, TOK], BF16)
        wbufs = [(sb(f"wb{i}", [128, KC, 512], BF16), f"wb{i}") for i in range(2)]
        ident = sb("ident", [128, 128], BF16)
        ones_bf = sb("ones_bf", [128, 128], BF16)
        mprev = sb("mprev", [128, 128], BF16)
        mnext = sb("mnext", [128, 128], BF16)
        maskL = sb("maskL", [128, 128], BF16)
        maskR = sb("maskR", [128, 128], BF16)
        Mf = sb("Mf", [128, 128], BF16)
        Mb = sb("Mb", [128, 128], BF16)
        chunkmask = sb("chunkmask", [128, 4], BF16)
        rmask = sb("rmask", [128, TOK], BF16)
        ropeC = sb("ropeC", [128, 8, 64], F32)
        ropeS = sb("ropeS", [128, 8, 64], F32)
        sel = sb("sel", [128, 32], F32)
        icnt = sb("icnt", [128, 2, 4, 16], F32)
        scT = sb("scT", [128, KC, 2], BF16)
        ccT = sb("ccT", [128, 2, KC], F32)
        lbT = sb("lbT", [128, DEPTH, 8], F32)
        omlT = sb("omlT", [128, DEPTH, 8], F32)
        lbtmp = sb("lbtmp", [128, DEPTH, 8], F32)
        lbsum = sb("lbsum", [128, 8], F32)
        expsink = sb("expsink", [128, DEPTH * 8], F32)
        kT_all = sb("kT_all", [128, 12, 128], BF16)
        Vaug = sb("Vaug", [128, 12, 2, 65], BF16)
        wukT = sb("wukT", [128, 4, 128], BF16)
        wuv = sb("wuv", [128, 4, 128], BF16)
        wpool = sb("wpool", [128, 4, 128], BF16)
        g_preT = sb("g_preT", [128, KC], F32)
        b_adaT = sb("b_adaT", [128, 48], F32)
        qnormT = sb("qnormT", [128, 3], F32)
        kvn_bc = sb("kvn_bc", [128, 128], F32)
        hnormT = sb("hnormT", [128, 4], F32)
        pscT = sb("pscT", [128, 4], F32)
        modT = sb("modT", [128, 32, 2], F32)
        gsT = sb("gsT", [128, KC, 2], F32)
        shT = sb("shT", [128, KC, 2], F32)
        small = sb("small", [128, 64], F32)
        RN = (int(nc.sbuf_bytes_remaining) - 1024) // 2 // 32 * 32
        print("RN", RN)
        Rt = sb("R", [128, RN], BF16)
        R = Region(Rt, RN, "R")

        psf = [pst(f"psf{i}", [128, 512], F32) for i in range(6)]
        psb = [pst(f"psb{i}", [128, 1024], BF16) for i in range(2)]
        rot = {"f": 0, "b": 0}

        def pf(idx=None):
            if idx is None:
                idx = rot["f"] % 6
                rot["f"] += 1
            return psf[idx], f"psf{idx}"

        def pb():
            i = rot["b"] % 2
            rot["b"] += 1
            return psb[i], f"psb{i}"

        with nc.Block() as block:
            T = Tracker(nc, es)
            ws = WStream(T, wbufs)

            def colblk(li, c0, w):
                src = w_in[li].rearrange("(kc p) c -> p kc c", p=128)[:, :, c0:c0 + w]
                return lambda buf: [(buf[:, :, 0:w], src)]

            def adablk(li, cb):
                src = w_ada[li].rearrange("(kc p) c -> p kc c", p=128)[:, :, cb * 512:(cb + 1) * 512]
                return lambda buf: [(buf[:, :, :], src)]

            def cheadblk(li, h):
                v = w_in[li].rearrange("(kc p) c -> p kc c", p=128)
                offs = [2880 + h * 128, 448 + h * 128, 960 + h * 128]
                return lambda buf: [(buf[:, :, j * 128:(j + 1) * 128], v[:, :, o:o + 128]) for j, o in enumerate(offs)]

            def mgblk(li, dc):
                v = w_in[li][:, MERGE0:].rearrange("(kc p) (n d) -> p kc n d", p=128, n=4)
                return lambda buf: [(buf[:, :, n * 128:(n + 1) * 128], v[:, :, n, dc * 128:(dc + 1) * 128]) for n in range(4)]

            def woutblk(li, cb):
                src = w_out[li].rearrange("(kc p) c -> p kc c", p=128)[:, :, cb * 512:(cb + 1) * 512]
                return lambda buf: [(buf[:, :, :], src)]

            def wbrblk(li, dc):
                src = w_branch[li].rearrange("n (wc p) d -> p (n wc) d", p=128)[:, :, dc * 128:(dc + 1) * 128]
                return lambda buf: [(buf[:, :, :], src)]

            for sg_, li in plan:
                if sg_ == "pre":
                    for cb in range(12):
                        ws.add(("ada", li, cb), adablk(li, cb))
                    ws.add(("kv", li), colblk(li, 0, 448))
                    ws.add(("aq", li), colblk(li, 1984, 512))
                    ws.add(("cq", li), colblk(li, 2496, 384))
                    ws.add(("dx", li), colblk(li, 3392, 512))
                    ws.add(("ci", li), colblk(li, 1472, 512))
                    for h in range(4):
                        ws.add(("chead", li, h), cheadblk(li, h))
                else:
                    for n in range(4):
                        ws.add(("gate", li, n), colblk(li, GATE0 + n * 512, 512))
                    for dc in range(16):
                        ws.add(("mg", li, dc), mgblk(li, dc))
                    for cb in range(4):
                        ws.add(("wout", li, cb), woutblk(li, cb))

            def tt(eng, out_, in0, in1, op, reads, writes):
                T.op(eng, lambda e: e.tensor_tensor(out=out_, in0=in0, in1=in1, op=op), reads, writes)

            def ts(eng, out_, in0, s1, s2, op0, op1, reads, writes):
                if op1 is None:
                    T.op(eng, lambda e: e.tensor_scalar(out=out_, in0=in0, scalar1=s1, scalar2=None, op0=op0), reads, writes)
                else:
                    T.op(eng, lambda e: e.tensor_scalar(out=out_, in0=in0, scalar1=s1, scalar2=s2, op0=op0, op1=op1), reads, writes)

            def stt(eng, out_, in0, scalar, in1, op0, op1, reads, writes):
                T.op(eng, lambda e: e.scalar_tensor_tensor(out=out_, in0=in0, scalar=scalar, in1=in1, op0=op0, op1=op1), reads, writes)

            def act(out_, in_, func, reads, writes, scale=1.0, bias=0.0, accum_out=None):
                if accum_out is None:
                    T.op("act", lambda e: e.activation(out=out_, in_=in_, func=func, bias=bias, scale=scale), reads, writes)
                else:
                    T.op("act", lambda e: e.activation(out=out_, in_=in_, func=func, bias=bias, scale=scale, accum_out=accum_out), reads, writes)

            def cp(eng, out_, in_, reads, writes):
                if eng == "act":
                    T.op("act", lambda e: e.copy(out=out_, in_=in_), reads, writes)
                else:
                    T.op(eng, lambda e: e.tensor_copy(out=out_, in_=in_), reads, writes)

            def mmgroup(items, reads, writes):
                def fn(e):
                    ins = None
                    for (o, l, r, st, sp_) in items:
                        ins = e.matmul(o, lhsT=l, rhs=r, start=st, stop=sp_, skip_group_check=True)
                    return ins
                T.op("pe", fn, reads, writes)

            def transposes(items, reads, writes):
                def fn(e):
                    ins = None
                    for (o, i_) in items:
                        ins = e.transpose(o, i_, ident[:])
                    return ins
                T.op("pe", fn, tuple(reads) + ("const",), writes)

            def rstd_from_ss(ss, n, tag):
                ts("dve", ss, ss, 1.0 / n, EPS, ALU.mult, ALU.add, (tag,), (tag,))
                T.op("act", lambda e: e.sqrt(out=ss, in_=ss), (tag,), (tag,))
                T.op("dve", lambda e: e.reciprocal(out=ss, in_=ss), (tag,), (tag,))

            ropec = [0]

            def rope(src, H, tile_idx, dst4, rd, wr, tmp1, tmp2):
                rk = ropec[0] % 3
                ropec[0] += 1
                tmp1, tmp2 = st["ropetmp"][rk]
                r1n, r2n = f"ropet1_{rk}", f"ropet2_{rk}"
                s4 = src.rearrange("p (h q s) -> p h q s", h=H, q=4)
                C4 = ropeC[:, tile_idx, :].rearrange("p (o q s) -> p o q s", o=1, q=4).to_broadcast([128, H, 4, 16])
                S4 = ropeS[:, tile_idx, :].rearrange("p (o q s) -> p o q s", o=1, q=4)
                t1 = tmp1[:, 0:H * 64].rearrange("p (h q s) -> p h q s", h=H, q=4)
                t2 = tmp2[:, 0:H * 64].rearrange("p (h q s) -> p h q s", h=H, q=4)
                tt("dve", t1, s4, C4, ALU.mult, rd + ("const",), (r1n,))
                tt("dve", t2[:, :, 0::2, :], s4[:, :, 1::2, :], S4[:, :, 0::2, :].to_broadcast([128, H, 2, 16]), ALU.mult,
                   rd + ("const",), (r2n,))
                tt("dve", t2[:, :, 1::2, :], s4[:, :, 0::2, :], S4[:, :, 1::2, :].to_broadcast([128, H, 2, 16]), ALU.mult,
                   rd + ("const",), (r2n,))
                tt("dve", dst4, t1, t2, ALU.add, (r1n, r2n), wr)

            def proj_tok(wb, wres, ncols, t, p, pres):
                mmgroup([(p[:, 0:ncols], st["hT"][:, kc, t * 128:(t + 1) * 128], wb[:, kc, 0:ncols], kc == 0, kc == KC - 1)
                         for kc in range(KC)], (wres, "hT"), (pres,))

            def proj_feat(wb, wres, j0, r0, n, p, pres):
                mmgroup([(p[:, 0:n], wb[:, kc, j0:j0 + 128], st["hT"][:, kc, r0:r0 + n], kc == 0, kc == KC - 1)
                         for kc in range(KC)], (wres, "hT"), (pres,))

            cm_f, _ = R.alloc([128, 6 * 128 + 4], F32, "cm_f")
            cm2_f, _ = R.alloc([128, 128 + TOK], F32, "cm2_f")
            T.dma("sp", cm_f[:], cmask_in[:, :], (), ("cmf",))
            T.dma("sp", cm2_f[:], cmask2_in[:, :], (), ("cmf2",))
            T.dma("sp", ropeC[:], ropeC_in[:, :, :], (), ("const",))
            T.dma("sp", ropeS[:], ropeS_in[:, :, :], (), ("const",))
            T.dma("sp", sel[:], sel_in[:, :], (), ("const",))
            T.dma("sp", icnt[:], icnt_in[:, :, :, :], (), ("const",))
            T.dma("sp", ccT[:], cc_in.rearrange("s (kc p) -> p s kc", p=128), (), ("ccT",), slow=True)
            T.dma("sp", expsink[:], a_sink.partition_broadcast(128).rearrange("p o f -> p (o f)"), (), ("expsink",))
            T.dma("sp", lbtmp[:], hgrn_lb.rearrange("l d (h p) -> p l (d h)", p=128), (), ("lbtmp",), slow=True)
            for i, t_ in enumerate((ident, mprev, mnext, maskL, maskR, Mf)):
                cp("dve", t_[:], cm_f[:, i * 128:(i + 1) * 128], ("cmf",), ("const",))
            cp("dve", chunkmask[:], cm_f[:, 768:772], ("cmf",), ("const",))
            cp("dve", Mb[:], cm2_f[:, 0:128], ("cmf2",), ("const",))
            cp("dve", rmask[:], cm2_f[:, 128:128 + TOK], ("cmf2",), ("const",))
            T.op("dve", lambda e: e.memset(ones_bf[:], 1.0), (), ("const",))
            T.op("dve", lambda e: e.memset(Vaug[:], 1.0), (), ("Vaug",))
            T.op("dve", lambda e: e.memset(kT_all[:], 0.0), (), ("kT_all",))
            act(scT[:].rearrange("p k s -> p s k"), ccT[:], AF.Silu, ("ccT",), ("scT",))
            act(expsink[:], expsink[:], AF.Exp, ("expsink",), ("expsink",))
            act(lbtmp[:], lbtmp[:], AF.Exp, ("lbtmp",), ("lbtmp",))
            T.op("dve", lambda e: e.tensor_copy(out=lbsum[:], in_=lbtmp[:, 0, :]), ("lbtmp",), ("lbsum",))
            for l in range(1, DEPTH):
                tt("dve", lbsum[:], lbsum[:], lbtmp[:, l, :], ALU.add, ("lbtmp", "lbsum"), ("lbsum",))
            T.op("dve", lambda e: e.reciprocal(out=lbsum[:], in_=lbsum[:]), ("lbsum",), ("lbsum",))
            T.op("dve", lambda e: e.memset(lbT[:], 0.0), (), ("lbT",))
            for l in range(1, DEPTH):
                tt("dve", lbT[:, l, :], lbT[:, l - 1, :], lbtmp[:, l, :], ALU.add, ("lbtmp", "lbT"), ("lbT",))
            for l in range(1, DEPTH):
                tt("dve", lbT[:, l, :], lbT[:, l, :], lbsum[:], ALU.mult, ("lbsum", "lbT"), ("lbT",))
            ts("dve", omlT[:], lbT[:], -1.0, 1.0, ALU.mult, ALU.add, ("lbT",), ("omlT",))

            st = {}
            xr_res = 'xs_r' if split else 'xs_d'
            xw_res = 'xs_w' if split else 'xs_d'

            def emit_loads(li):
                T.barrier()
                R.reset()
                T.dma("pool", wukT[:], w_ukT[li].rearrange("h n l -> n h l"), (), ("wukT",))
                T.dma("pool", wuv[:], w_ukv[li].rearrange("l (h two v) -> l h two v", h=4, two=2)[:, :, 1, :], (), ("wuv",))
                T.dma("pool", wpool[:], w_pool[li].rearrange("g i o -> i g o"), (), ("wpool",))
                T.dma("sp", g_preT[:], g_pre[li].rearrange("o (kc p) -> p (o kc)", p=128), (), ("g_preT",), slow=True)
                T.dma("sp", b_adaT[:], b_ada[li].rearrange("o (ch p) -> p (o ch)", p=128), (), ("b_adaT",), slow=True)
                T.dma("sp", qnormT[:], q_norm[li].rearrange("o (kc p) -> p (o kc)", p=128), (), ("qnormT",), slow=True)
                T.dma("sp", kvn_bc[:], kv_norm[li][0:1, :].partition_broadcast(128).rearrange("p o f -> p (o f)"), (), ("kvn_bc",))
                T.dma("sp", hnormT[:], hgrn_norm[li].rearrange("o (h p) -> p (o h)", p=128), (), ("hnormT",), slow=True)
                T.dma("sp", pscT[:], pool_scale[li].rearrange("o (h p) -> p (o h)", p=128), (), ("pscT",), slow=True)

            def emit_S1(li, from_w):
                T.barrier()
                R.reset()
                hT, _ = R.alloc([128, KC, TOK], BF16, "hT")
                hT_off = R.off
                xb = [R.alloc([128, D], F32, f"xb{i}") for i in range(2)]
                xnb = [R.alloc([128, D], BF16, f"xn{i}") for i in range(2)]
                htmp, htmp_r = R.alloc([128, 8, 128], F32, "htmp")
                junk, junk_r = R.alloc([128, D], BF16, "junk")
                for t in range(NT):
                    s = 0 if t < 8 else 1
                    xt, xr = xb[t % 2]
                    xn, xnr = xnb[t % 2]
                    if li == 0:
                        src = x_in[t * 128:(t + 1) * 128, :] if t < 8 else ctx_in[(t - 8) * 128:(t - 7) * 128, :]
                        T.dma("sp", xt, src, (), (xr,))
                    elif from_w:
                        T.dma("sp", xt, xs_w[t * 128:(t + 1) * 128, :], (xw_res,), (xr,))
                    else:
                        T.dma("sp", xt, xs_r[t * 128:(t + 1) * 128, :], (xr_res,), (xr,))
                    ss = small[:, t:t + 1]
                    T.op("dve", lambda e: e.memset(ss, 0.0), (), (f"ss{t}",))
                    act(junk, xt, AF.Square, (xr,), (junk_r, f"ss{t}"), accum_out=ss)
                    rstd_from_ss(ss, D, f"ss{t}")
                    ts("dve", xn, xt, ss, None, ALU.mult, None, (xr, f"ss{t}"), (xnr,))
                    for half in range(2):
                        p, pr = pb()
                        transposes([(p[:, k * 128:(k + 1) * 128], xn[:, (half * 8 + k) * 128:(half * 8 + k + 1) * 128]) for k in range(8)],
                                   (xnr,), (pr,))
                        tt("dve", htmp, p[:, :].rearrange("p (k c) -> p k c", k=8),
                           gsT[:, half * 8:half * 8 + 8, s:s + 1].to_broadcast([128, 8, 128]), ALU.mult, (pr, "gsT"), (htmp_r,))
                        tt("dve", hT[:, half * 8:half * 8 + 8, t * 128:(t + 1) * 128], htmp,
                           shT[:, half * 8:half * 8 + 8, s:s + 1].to_broadcast([128, 8, 128]), ALU.add, (htmp_r, "shT"), ("hT",))
                if dbg and li == 0:
                    T.dma("sp", dbg_h[:, :, :], hT, ("hT",), ("dbg_h",))

                st['hT'] = hT
                st['hT_off'] = hT_off

            def pre(li):
                emit_loads(li)
                cbb, cbb_r = R.alloc([128, KC, 2, 128], BF16, "cbb")
                bg_bc, bg_r = R.alloc([128, D], F32, "bg_bc")
                gp_bc, gp_r = R.alloc([128, D], F32, "gp_bc")
                ggs, ggs_r = R.alloc([128, 2, 512], F32, "ggs")
                T.dma("sp", bg_bc, b_ada[li][0:1, 2 * D:3 * D].partition_broadcast(128).rearrange("p o f -> p (o f)"), (), (bg_r,))
                T.dma("sp", gp_bc, g_post[li][0:1, :].partition_broadcast(128).rearrange("p o f -> p (o f)"), (), (gp_r,))
                cp("dve", cbb, scT[:].rearrange("p k (s o) -> p k s o", o=1).to_broadcast([128, KC, 2, 128]), ("scT",), (cbb_r,))
                pA, pA_r = pf()
                for cb in range(8):
                    wb, wres = ws.next(("ada", li, cb))
                    for j in range(4):
                        ch = cb * 4 + j
                        mmgroup([(pA[:, ch * 2:ch * 2 + 2], wb[:, kc, j * 128:(j + 1) * 128], scT[:, kc, :], kc == 0, kc == KC - 1)
                                 for kc in range(KC)], (wres, "scT"), (pA_r,))
                tt("dve", modT[:], pA[:, 0:64].rearrange("p (c s) -> p c s", s=2),
                   b_adaT[:, 0:32].rearrange("p (c o) -> p c o", o=1).to_broadcast([128, 32, 2]), ALU.add,
                   (pA_r, "b_adaT"), ("modT",))
                ts("dve", gsT[:], modT[:, 16:32, :], 1.0, None, ALU.add, None, ("modT",), ("gsT",))
                tt("dve", gsT[:], gsT[:], g_preT[:].rearrange("p (c o) -> p c o", o=1).to_broadcast([128, KC, 2]), ALU.mult,
                   ("gsT", "g_preT"), ("gsT",))
                cp("dve", shT[:], modT[:, 0:16, :], ("modT",), ("shT",))
                for cb in range(4):
                    wb, wres = ws.next(("ada", li, 8 + cb))
                    for s in range(2):
                        p, pr = pf()
                        mmgroup([(p[:, :], cbb[:, kc, s, :], wb[:, kc, :], kc == 0, kc == KC - 1) for kc in range(KC)],
                                (wres, cbb_r), (pr,))
                        tt("dve", ggs[:, s, :], p[:, :], bg_bc[:, cb * 512:(cb + 1) * 512], ALU.add, (pr, bg_r), (ggs_r,))
                        tt("dve", ggs[:, s, :], ggs[:, s, :], gp_bc[:, cb * 512:(cb + 1) * 512], ALU.mult, (ggs_r, gp_r), (ggs_r,))
                    T.dma("sp", DW["gg_d"][:, :, cb * 512:(cb + 1) * 512].rearrange("s p c -> p s c"), ggs, (ggs_r,), ("gg_d",))

                if split:
                    T.dma('sp', DW['mod_d'][:, 0:32], gsT[:].rearrange('p k s -> p (k s)'), ('gsT',), ('mod_d',))
                    T.dma('sp', DW['mod_d'][:, 32:64], shT[:].rearrange('p k s -> p (k s)'), ('shT',), ('mod_d',))
                emit_S1(li, True)
                hT = st['hT']
                hT_off = st['hT_off']
                T.barrier()
                R.reset(hT_off)
                tmp1, _ = R.alloc([128, 512], F32, "tmp1")
                tmp2, _ = R.alloc([128, 512], F32, "tmp2")
                st["ropetmp"] = [(R.alloc([128, 512], F32, f"rt1_{i}")[0], R.alloc([128, 512], F32, f"rt2_{i}")[0]) for i in range(3)]
                ksb = [R.alloc([128, 128], BF16, f"ksb{i}") for i in range(2)]
                stg = [R.alloc([128, 3, 128], BF16, f"stg{i}") for i in range(2)]
                krd = [R.alloc([128, 2, 64], BF16, f"krd{i}") for i in range(2)]
                ckf, ckf_r = R.alloc([128, 128], F32, "ckf")
                qperm = [R.alloc([128, 4, 2, 64], BF16, f"qperm{i}") for i in range(2)]
                qTs = [R.alloc([128, 512], BF16, f"qTs{i}") for i in range(2)]
                cqn = [R.alloc([128, 384], BF16, f"cqn{i}") for i in range(2)]
                cqT, cqT_r = R.alloc([128, 3, TOK], BF16, "cqT")
                wuqr, wuqr_r = R.alloc([128, 3, 4, 64], BF16, "wuqr")
                wuq, _ = R.alloc([128, 3, 768], BF16, "wuq")
                T.dma("pool", wuq, w_uq[li].rearrange("(kc p) c -> p kc c", p=128), (), ("wuq",))
                qn_sb = [R.alloc([128, 512], BF16, f"qn{i}") for i in range(2)]
                qa_sb = [R.alloc([128, 512], BF16, f"qa{i}") for i in range(2)]
                qr_sb = [R.alloc([128, 4, 64], BF16, f"qr{i}") for i in range(2)]
                qrT_sb = [R.alloc([128, 2, 128], BF16, f"qrT{i}") for i in range(2)]
                for kc_ in range(3):
                    T.dma("pool", wuqr[:, kc_, :, :], w_uq[li][kc_ * 128:(kc_ + 1) * 128, :].rearrange("p (h c) -> p h c", h=4)[:, :, 128:192],
                          (), (wuqr_r,))

                wb, wres = ws.next(("kv", li))
                for t in range(NT):
                    own = t < 8
                    slot = t + 1 if own else t + 2
                    p, pr = pf()
                    proj_tok(wb, wres, 448, t, p, pr)
                    kb, kbr = ksb[t % 2]
                    if own:
                        rope(p[:, 0:128], 2, t, kb.rearrange("p (h q s) -> p h q s", h=2, q=4), (pr,), (kbr,), tmp1, tmp2)
                    else:
                        cp("act", kb, p[:, 0:128], (pr,), (kbr,))
                    pt, ptr = pb()
                    transposes([(pt[:, 0:128], kb)], (kbr,), (ptr,))
                    cp("act", kT_all[:, slot, :], pt[:, 0:128], (ptr,), ("kT_all",))
                    cp("act", Vaug[:, slot, :, 0:64], p[:, 128:256].rearrange("p (g d) -> p g d", g=2), (pr,), ("Vaug",))
                    st_, str_ = stg[t % 2]
                    ssk = small[:, 16 + t:17 + t]
                    T.op("dve", lambda e: e.memset(ssk, 0.0), (), (f"ssk{t}",))
                    act(ckf, p[:, 256:384], AF.Square, (pr,), (ckf_r, f"ssk{t}"), accum_out=ssk)
                    rstd_from_ss(ssk, 128, f"ssk{t}")
                    stt("dve", st_[:, 2, :], p[:, 256:384], ssk, kvn_bc[:], ALU.mult, ALU.mult, (pr, f"ssk{t}", "kvn_bc"), (str_,))
                    kd, kdr = krd[t % 2]
                    if own:
                        rope(p[:, 384:448], 1, t, kd[:, 0:1, :].rearrange("p h (q s) -> p h q s", q=4), (pr,), (kdr,), tmp1, tmp2)
                    else:
                        cp("act", kd[:, 0, :], p[:, 384:448], (pr,), (kdr,))
                    cp("dve", kd[:, 1, :], kd[:, 0, :], (kdr,), (kdr,))
                    pt2, ptr2 = pb()
                    transposes([(pt2[:, 0:128], st_[:, 2, :]), (pt2[:, 128:256], kd.rearrange("p a b -> p (a b)"))], (str_, kdr), (ptr2,))
                    cp("act", st_[:, 0:2, :], pt2[:, 0:256].rearrange("p (a b) -> p a b", a=2), (ptr2,), (str_,))
                    if own:
                        T.dma("sp", xinAB[:, t * 384:(t + 1) * 384], st_.rearrange("p a b -> p (a b)"), (str_,), ("xinAB",))
                    else:
                        T.dma("sp", DW["kvctx_d"][:, t - 8, :, :], st_, (str_,), ("kvctx_d",))
                    if t == 0:
                        T.dma("sp", xinAB[:, 3072:3200], kT_all[:, 1, :], ("kT_all",), ("xinAB",))
                        T.dma("sp", xinAB[:, 3200:3330], Vaug[:, 1, :, :].rearrange("p g d -> p (g d)"), ("Vaug",), ("xinAB",))
                    if t == 7:
                        T.dma("sp", xinAB[:, 3330:3458], kT_all[:, 8, :], ("kT_all",), ("xinAB",))
                        T.dma("sp", xinAB[:, 3458:3588], Vaug[:, 8, :, :].rearrange("p g d -> p (g d)"), ("Vaug",), ("xinAB",))

                wb, wres = ws.next(("aq", li))
                for t in range(NT):
                    own = t < 8
                    p, pr = pf()
                    proj_tok(wb, wres, 512, t, p, pr)
                    qp, qpr = qperm[t % 2]
                    for g in range(2):
                        dst = qp[:, :, g, :].rearrange("p j (q s) -> p j q s", q=4)
                        if own:
                            rope(p[:, g * 256:(g + 1) * 256], 4, t, dst, (pr,), (qpr,), tmp1, tmp2)
                        else:
                            cp("act", qp[:, :, g, :], p[:, g * 256:(g + 1) * 256].rearrange("p (j d) -> p j d", j=4), (pr,), (qpr,))
                    pt, ptr = pb()
                    transposes([(pt[:, j * 128:(j + 1) * 128], qp[:, j, :, :].rearrange("p g d -> p (g d)")) for j in range(4)], (qpr,), (ptr,))
                    qs, qsr = qTs[t % 2]
                    cp("act", qs, pt[:, 0:512], (ptr,), (qsr,))
                    T.dma("sp", DW["qTA_d"][t], qs, (qsr,), ("qTA_d",))

                wb, wres = ws.next(("cq", li))
                for t in range(NT):
                    p, pr = pf()
                    proj_tok(wb, wres, 384, t, p, pr)
                    ssq = small[:, 32 + t:33 + t]
                    T.op("dve", lambda e: e.memset(ssq, 0.0), (), (f"ssq{t}",))
                    act(tmp1[:, 0:384], p[:, 0:384], AF.Square, (pr,), ("ropet1", f"ssq{t}"), accum_out=ssq)
                    rstd_from_ss(ssq, 384, f"ssq{t}")
                    cn, cnr = cqn[t % 2]
                    ts("dve", cn, p[:, 0:384], ssq, None, ALU.mult, None, (pr, f"ssq{t}"), (cnr,))
                    pt, ptr = pb()
                    transposes([(pt[:, k * 128:(k + 1) * 128], cn[:, k * 128:(k + 1) * 128]) for k in range(3)], (cnr,), (ptr,))
                    tt("dve", cqT[:, :, t * 128:(t + 1) * 128], pt[:, 0:384].rearrange("p (k c) -> p k c", k=3),
                       qnormT[:].rearrange("p (k o) -> p k o", o=1).to_broadcast([128, 3, 128]), ALU.mult, (ptr, "qnormT"), (cqT_r,))
                it = 0
                for h in range(4):
                    for (r0, n) in RANGES:
                        p, pr = pf()
                        mmgroup([(p[:, 0:n], wuq[:, kc, h * 192:h * 192 + 128], cqT[:, kc, r0:r0 + n], kc == 0, kc == 2) for kc in range(3)],
                                ("wuq", cqT_r), (pr,))
                        qn, qnr = qn_sb[it % 2]
                        cp("act", qn[:, 0:n], p[:, 0:n], (pr,), (qnr,))
                        p2, pr2 = pf()
                        mmgroup([(p2[:, 0:n], wukT[:, h, :], qn[:, 0:n], True, True)], ("wukT", qnr), (pr2,))
                        qa, qar = qa_sb[it % 2]
                        cp("act", qa[:, 0:n], p2[:, 0:n], (pr2,), (qar,))
                        T.dma("sp", DW["qabs_d"][h, :, r0:r0 + n], qa[:, 0:n], (qar,), ("qabs_d",))
                        it += 1
                for t in range(NT):
                    own = t < 8
                    p, pr = pf()
                    mmgroup([(p[:, 0:256], cqT[:, kc, t * 128:(t + 1) * 128], wuqr[:, kc, :, :].rearrange("p h c -> p (h c)"), kc == 0, kc == 2)
                             for kc in range(3)], (wuqr_r, cqT_r), (pr,))
                    qr, qrr = qr_sb[t % 2]
                    if own:
                        rope(p[:, 0:256], 4, t, qr.rearrange("p h (q s) -> p h q s", q=4), (pr,), (qrr,), tmp1, tmp2)
                    else:
                        cp("act", qr.rearrange("p h d -> p (h d)"), p[:, 0:256], (pr,), (qrr,))
                    pt, ptr = pb()
                    transposes([(pt[:, i * 128:(i + 1) * 128], qr[:, 2 * i:2 * i + 2, :].rearrange("p a b -> p (a b)")) for i in range(2)], (qrr,), (ptr,))
                    qt_, qtr = qrT_sb[t % 2]
                    cp("act", qt_, pt[:, 0:256].rearrange("p (a b) -> p a b", a=2), (ptr,), (qtr,))
                    T.dma("sp", DW["qrT_d"][:, :, t * 128:(t + 1) * 128], qt_, (qtr,), ("qrT_d",))

                if not split:
                    T.collective(xinAB[:, :], xoutAB[:, :], ("xinAB",), ("xoutAB",))

                T.barrier(("pe", "act", "dve", "sp"))
                R.reset(hT_off)
                uo, uo_r = R.alloc([128, 4, UW], F32, "uo")
                uc, uc_r = R.alloc([128, 4, UCW], F32, "uc")
                dstg, dstg_r = R.alloc([128, 2, 4, 8], F32, "dstg")
                T.op("dve", lambda e: e.memset(uc, 0.0), (), (uc_r,))
                T.op("dve", lambda e: e.memset(uo, 0.0), (), (uo_r,))
                wb, wres = ws.next(("dx", li))
                for g in range(4):
                    for (r0, n) in RANGES:
                        p, pr = pf()
                        proj_feat(wb, wres, g * 128, r0, n, p, pr)
                        if r0 < OWN:
                            cp("act", uo[:, g, 8 + r0:8 + r0 + n], p[:, 0:n], (pr,), (uo_r,))
                        else:
                            cp("act", uc[:, g, 8:8 + n], p[:, 0:n], (pr,), (uc_r,))
                cp("dve", dstg[:, 0, :, :], uo[:, :, 8:16], (uo_r,), (dstg_r,))
                cp("dve", dstg[:, 1, :, :], uo[:, :, OWN:OWN + 8], (uo_r,), (dstg_r,))
                T.dma("sp", xinCD[:, 8 * 129:8 * 129 + 64], dstg.rearrange("p a g e -> p (a g e)"), (dstg_r,), ("xinCD",))
                T.dma("sp", DW["u_d"][:, :, :], uo, (uo_r,), ("u_d",))
                T.dma("sp", DW["uc_d"][:, :, :], uc, (uc_r,), ("uc_d",))

                T.barrier(("pe", "act", "dve", "sp"))
                R.reset(hT_off)
                v_all, v_r = R.alloc([128, NT, 512], BF16, "v_all")
                sq, sq_r = R.alloc([128, TOK], F32, "sq")
                T1, T1_r = R.alloc([128, TOK], F32, "T1")
                T2, T2_r = R.alloc([128, TOK], F32, "T2")
                T3, T3_r = R.alloc([128, TOK], F32, "T3")
                osum, os_r = R.alloc([128, TOK], F32, "osum")
                qt2 = [R.alloc([128, TOK], BF16, f"qt{i}") for i in range(2)]
                kt2 = [R.alloc([128, TOK], BF16, f"kt{i}") for i in range(2)]
                kh, kh_r = R.alloc([128, TOK], BF16, "kh")
                khtok2 = [R.alloc([128, NT, 128], BF16, f"khtok{i}") for i in range(2)]
                Qh, Qh_r = R.alloc([128, OWN], BF16, "Qh")
                Vblk = [R.alloc([128, 4, 128], BF16, f"Vblk{i}") for i in range(4)]
                vbi = 0
                totc, totc_r = R.alloc([128, NCHK], F32, "totc")
                achk2 = [R.alloc([128, NCHK], F32, f"achk{i}") for i in range(2)]
                Sst = [R.alloc([128, 128], F32, f"S{i}") for i in range(8)]
                Sbf = [R.alloc([128, 4, 128], BF16, f"Sbf{i}") for i in range(4)]
                attm = [R.alloc([128, 128], BF16, f"attm{i}") for i in range(4)]
                cstg, cstg_r = R.alloc([128, 8, 129], F32, "cstg")
                gtot, gtot_r = R.alloc([128, 2], F32, "gtot")

                wb, wres = ws.next(("ci", li))
                for t in range(NT):
                    p, pr = pf()
                    proj_tok(wb, wres, 512, t, p, pr)
                    cp("act", v_all[:, t, :], p[:, :], (pr,), (v_r,))
                sbi = 0
                ami = 0
                for h in range(4):
                    wb, wres = ws.next(("chead", li, h))
                    for (r0, n) in RANGES:
                        p, pr = pf()
                        proj_feat(wb, wres, 0, r0, n, p, pr)
                        act(sq[:, r0:r0 + n], p[:, 0:n], AF.Silu, (pr,), (sq_r,))
                    for d in range(2):
                        hd = d * 4 + h
                        qt, qt_r = qt2[d]
                        kt, kt_r = kt2[d]
                        khtok, khtok_r = khtok2[d]
                        achk, achk_r = achk2[d]
                        for (r0, n) in RANGES:
                            p, pr = pf()
                            proj_feat(wb, wres, 128 * (1 + d), r0, n, p, pr)
                            act(T1[:, r0:r0 + n], p[:, 0:n], AF.Sigmoid, (pr,), (T1_r,))
                        ts("dve", T1, T1, omlT[:, li, hd:hd + 1], lbT[:, li, hd:hd + 1], ALU.mult, ALU.add, (T1_r, "omlT", "lbT"), (T1_r,))
                        act(T2, T1, AF.Ln, (T1_r,), (T2_r,))
                        ts("dve", T1, T1, -1.0, 1.0, ALU.mult, ALU.add, (T1_r,), (T1_r,))
                        T.op("dve", lambda e: e.tensor_tensor_scan(out=T3[:, 0:OWN], data0=ones_bf[:, 0:1].to_broadcast([128, OWN]), data1=T2[:, 0:OWN], initial=0.0,
                                                                    op0=ALU.mult, op1=ALU.add), (T2_r, "const"), (T3_r,))
                        cp("dve", gtot[:, 0:1], T3[:, OWN - 1:OWN], (T3_r,), (gtot_r,))
                        if d == 1:
                            tt("dve", T3[:, 0:OWN], T2[:, 0:OWN], T3[:, 0:OWN], ALU.subtract, (T2_r, T3_r), (T3_r,))
                            ts("dve", T3[:, 0:OWN], T3[:, 0:OWN], gtot[:, 0:1], None, ALU.add, None, (T3_r, gtot_r), (T3_r,))
                        act(T3[:, 0:OWN], T3[:, 0:OWN], AF.Exp, (T3_r,), (T3_r,))
                        stt("dve", Qh, T3[:, 0:OWN], 128.0 ** -0.5, sq[:, 0:OWN], ALU.mult, ALU.mult, (T3_r, sq_r), (Qh_r,))
                        T.dma("sp", DW["Qh_d"][hd], Qh, (Qh_r,), ("Qh_d",))
                        act(cstg[:, hd, 0:1], gtot[:, 0:1], AF.Exp, (gtot_r,), (cstg_r,))
                        T.op("dve", lambda e: e.tensor_tensor_scan(out=T3[:, :], data0=rmask[:], data1=T2[:, :], initial=0.0,
                                                                    op0=ALU.mult, op1=ALU.add), (T2_r, "const", T3_r), (T3_r,))
                        cp("dve", totc, T3.rearrange("p (c s) -> p c s", s=CH)[:, :, CH - 1], (T3_r,), (totc_r,))
                        totbc = totc.rearrange("p (c o) -> p c o", o=1).to_broadcast([128, NCHK, CH])
                        T3v = T3.rearrange("p (c s) -> p c s", s=CH)
                        T2v = T2.rearrange("p (c s) -> p c s", s=CH)
                        if d == 1:
                            tt("dve", T3, T2, T3, ALU.subtract, (T2_r, T3_r), (T3_r,))
                            tt("dve", T3v, T3v, totbc, ALU.add, (T3_r, totc_r), (T3_r,))
                        act(achk, totc, AF.Exp, (totc_r,), (achk_r,))
                        act(T2, T3, AF.Exp, (T3_r,), (T2_r,))
                        stt("dve", qt, T2, 128.0 ** -0.5, sq, ALU.mult, ALU.mult, (T2_r, sq_r), (qt_r,))
                        act(T2, T3, AF.Exp, (T3_r, qt_r), (T2_r,), scale=-1.0)
                        tt("dve", kt, T1, T2, ALU.mult, (T1_r, T2_r), (kt_r,))
                        tt("dve", T2v, totbc, T3v, ALU.subtract, (totc_r, T3_r, kt_r), (T2_r,))
                        act(T2, T2, AF.Exp, (T2_r,), (T2_r,))
                        tt("dve", kh, T1, T2, ALU.mult, (T1_r, T2_r), (kh_r,))
                        for t0 in (0, 8):
                            nt_ = 8 if t0 == 0 else 2
                            pt, ptr = pb()
                            transposes([(pt[:, k * 128:(k + 1) * 128], kh[:, (t0 + k) * 128:(t0 + k + 1) * 128]) for k in range(nt_)], (kh_r,), (ptr,))
                            cp("act", khtok[:, t0:t0 + nt_, :], pt[:, 0:nt_ * 128].rearrange("p (a b) -> p a b", a=nt_), (ptr,), (khtok_r,))
                    Scur = {}
                    for d_ in range(2):
                        for seg_ in range(2):
                            S0_, S0r_ = Sst[d_ * 4 + seg_ * 2]
                            T.op("dve", lambda e: e.memset(S0_, 0.0), (), (S0r_,))
                            Scur[(d_, seg_)] = [S0_, S0r_, 0]
                    written = set()

                    def tile_step(d, seg, t):
                        nonlocal vbi, ami, sbi
                        hd = d * 4 + h
                        qt, qt_r = qt2[d]
                        kt, kt_r = kt2[d]
                        khtok, khtok_r = khtok2[d]
                        achk, achk_r = achk2[d]
                        vb, vbr = Vblk[vbi % len(Vblk)]
                        vbi += 1
                        tt("dve", vb, v_all[:, t, h * 128:(h + 1) * 128].rearrange("p (o d) -> p o d", o=1).to_broadcast([128, 4, 128]),
                           chunkmask[:].rearrange("p (c o) -> p c o", o=1).to_broadcast([128, 4, 128]), ALU.mult, (v_r, "const"), (vbr,))
                        pU, pUr = pf()
                        mmgroup([(pU[:, :], khtok[:, t, :], vb.rearrange("p c d -> p (c d)"), True, True)],
                                (khtok_r, vbr), (pUr,))
                        pAt, pAtr = pf()
                        mmgroup([(pAt[:, 0:128], kt[:, t * 128:(t + 1) * 128], qt[:, t * 128:(t + 1) * 128], True, True)],
                                (kt_r, qt_r), (pAtr,))
                        am, amr = attm[ami % len(attm)]
                        ami += 1
                        tt("dve", am, pAt[:, 0:128], (Mf if d == 0 else Mb)[:], ALU.mult, (pAtr, "const"), (amr,))
                        sb_, sbr = Sbf[sbi % len(Sbf)]
                        sbi += 1
                        S, S_r, spp = Scur[(d, seg)]
                        corder = range(4) if d == 0 else range(3, -1, -1)
                        for c in corder:
                            cp("act", sb_[:, c, :], S, (S_r,), (sbr,))
                            spp ^= 1
                            S2, S2_r = Sst[d * 4 + seg * 2 + spp]
                            stt("dve", S2, S, achk[:, t * 4 + c:t * 4 + c + 1], pU[:, c * 128:(c + 1) * 128], ALU.mult, ALU.add,
                                (S_r, achk_r, pUr), (S2_r,))
                            S, S_r = S2, S2_r
                        Scur[(d, seg)] = [S, S_r, spp]
                        pO, pOr = pf()
                        items = [(pO[:, 0:128], v_all[:, t, h * 128:(h + 1) * 128], am, True, False)]
                        items += [(pO[:, c * CH:(c + 1) * CH], sb_[:, c, :], qt[:, t * 128 + c * CH:t * 128 + (c + 1) * CH], False, c == 3)
                                  for c in range(4)]
                        mmgroup(items, (sbr, qt_r, v_r, amr), (pOr,))
                        if t not in written:
                            written.add(t)
                            cp("act", osum[:, t * 128:(t + 1) * 128], pO[:, 0:128], (pOr,), (os_r,))
                        else:
                            tt("dve", osum[:, t * 128:(t + 1) * 128], osum[:, t * 128:(t + 1) * 128], pO[:, 0:128], ALU.add,
                               (pOr, os_r), (os_r,))

                    for idx in range(8):
                        tile_step(0, 0, idx)
                        tile_step(1, 0, 7 - idx)
                    for idx in range(2):
                        tile_step(0, 1, 8 + idx)
                        tile_step(1, 1, 9 - idx)
                    for d_ in range(2):
                        hd_ = d_ * 4 + h
                        cp("act", cstg[:, hd_, 1:129], Scur[(d_, 0)][0], (Scur[(d_, 0)][1],), (cstg_r,))
                        T.dma("sp", DW["sc_d"][hd_], Scur[(d_, 1)][0], (Scur[(d_, 1)][1],), ("sc_d",))
                    T.dma("sp", DW["osum_d"][h], osum, (os_r,), ("osum_d",))
                T.dma("sp", xinCD[:, 0:8 * 129], cstg.rearrange("p a b -> p (a b)"), (cstg_r,), ("xinCD",))
                if not split:
                    T.collective(xinCD[:, :], xoutCD[:, :], ("xinCD",), ("xoutCD",))

                if split:
                    T.dma('sp', DW['kv_d'][:, 0:12 * 128], kT_all[:].rearrange('p a b -> p (a b)'), ('kT_all',), ('kv_d',))
                    T.dma('sp', DW['kv_d'][:, 12 * 128:12 * 128 + 12 * 130], Vaug[:].rearrange('p a g d -> p (a g d)'), ('Vaug',), ('kv_d',))

            def post(li):
                last = (li == final_li)
                skipc = (li == DEPTH - 1)
                NTP = 8 if skipc else NT
                RNG = RANGES[:2] if skipc else RANGES
                if split:
                    T.barrier()
                    R.reset()
                    emit_loads(li)
                    T.dma('sp', gsT[:].rearrange('p k s -> p (k s)'), DR['mod_d'][:, 0:32], (), ('gsT',))
                    T.dma('sp', shT[:].rearrange('p k s -> p (k s)'), DR['mod_d'][:, 32:64], (), ('shT',))
                    T.dma('sp', kT_all[:].rearrange('p a b -> p (a b)'), DR['kv_d'][:, 0:12 * 128], (), ('kT_all',))
                    T.dma('sp', Vaug[:].rearrange('p a g d -> p (a g d)'), DR['kv_d'][:, 12 * 128:12 * 128 + 12 * 130], (), ('Vaug',))
                    emit_S1(li, False)
                hT = st['hT']
                hT_off = st['hT_off']
                T.barrier(("pe", "act", "dve", "sp"))
                R.reset(hT_off)
                gT, gT_r = R.alloc([128, 4, TOK], BF16, "gT")
                hal, hal_r = R.alloc([128, 8, 516], BF16, "hal")
                accL, accL_r = R.alloc([128, 258], F32, "accL")
                accR, accR_r = R.alloc([128, 258], F32, "accR")
                qTl = [R.alloc([128, 512], BF16, f"qTl{i}") for i in range(3)]
                Pb = [R.alloc([128, 512], BF16, f"P{i}") for i in range(22)]
                den = [R.alloc([128, 8], F32, f"den{i}") for i in range(2)]
                oa = [R.alloc([128, 8, 64], BF16, f"oa{i}") for i in range(2)]

                def load_gate(n):
                    wb, wres = ws.next(("gate", li, n))
                    for j in range(4):
                        for (r0, nn) in RNG:
                            p, pr = pf()
                            proj_feat(wb, wres, j * 128, r0, nn, p, pr)
                            act(gT[:, j, r0:r0 + nn], p[:, 0:nn], AF.Silu, (pr,), (gT_r,))

                T.dma("sp", hal, xoutAB.rearrange("(r p) w -> p r w", p=128)[:, :, 3072:3588], ("xoutAB",), (hal_r,))
                T.op("dve", lambda e: e.memset(accL, 0.0), (), (accL_r,))
                T.op("dve", lambda e: e.memset(accR, 0.0), (), (accR_r,))
                for j in range(NCORES):
                    stt("dve", accL, hal[:, j, 258:516], sel[:, j:j + 1], accL, ALU.mult, ALU.add, (hal_r, "const", accL_r), (accL_r,))
                    stt("dve", accR, hal[:, j, 0:258], sel[:, 8 + j:9 + j], accR, ALU.mult, ALU.add, (hal_r, "const", accR_r), (accR_r,))
                cp("dve", kT_all[:, 0, :], accL[:, 0:128], (accL_r,), ("kT_all",))
                cp("dve", Vaug[:, 0, :, :].rearrange("p g d -> p (g d)"), accL[:, 128:258], (accL_r,), ("Vaug",))
                cp("dve", kT_all[:, 9, :], accR[:, 0:128], (accR_r,), ("kT_all",))
                cp("dve", Vaug[:, 9, :, :].rearrange("p g d -> p (g d)"), accR[:, 128:258], (accR_r,), ("Vaug",))
                load_gate(0)
                pi = 0
                TL = list(range(NTP))

                def a_stage1(t):
                    nonlocal pi
                    own = t < 8
                    ql, qlr = qTl[t % 3]
                    T.dma("sp", ql, DR["qTA_d"][t], ("qTA_d",), (qlr,))
                    if own:
                        slots = [t, t + 1, t + 2, 10, 11]
                        masks = [maskL if t == 0 else mprev, None, maskR if t == 7 else mnext, None, None]
                    else:
                        slots = [10, 11]
                        masks = [None, None]
                    allP = []
                    for g in range(2):
                        Ps = []
                        for ki, slot in enumerate(slots):
                            p, pr = pf()
                            mmgroup([(p[:, :], kT_all[g * 64:(g + 1) * 64, slot, :], ql[g * 64:(g + 1) * 64, :], True, True)],
                                    ("kT_all", qlr), (pr,))
                            P_, Pr = Pb[pi % len(Pb)]
                            pi += 1
                            act(P_, p[:, :], AF.Exp, (pr,), (Pr,), scale=0.125)
                            if masks[ki] is not None:
                                tt("dve", P_.rearrange("p (j q) -> p j q", j=4), P_.rearrange("p (j q) -> p j q", j=4),
                                   masks[ki][:].rearrange("p (o q) -> p o q", o=1).to_broadcast([128, 4, 128]), ALU.mult, (Pr, "const"), (Pr,))
                            Ps.append((P_, Pr))
                        allP.append(Ps)
                    return slots, allP

                def a_stage2(t, slots, allP):
                    for g in range(2):
                        Ps = allP[g]
                        pO, pOr = pf()
                        items = []
                        for j in range(4):
                            for ki, slot in enumerate(slots):
                                items.append((pO[:, j * 65:(j + 1) * 65], Ps[ki][0][:, j * 128:(j + 1) * 128], Vaug[:, slot, g, :],
                                              ki == 0, ki == len(slots) - 1))
                        mmgroup(items, tuple(x_[1] for x_ in Ps) + ("Vaug",), (pOr,))
                        pO3 = pO[:, 0:260].rearrange("p (j d) -> p j d", j=4)
                        dn, dnr = den[t % 2]
                        o_, o_r = oa[t % 2]
                        tt("dve", dn[:, g * 4:(g + 1) * 4], pO3[:, :, 64], expsink[:, li * 8 + g * 4:li * 8 + g * 4 + 4], ALU.add,
                           (pOr, "expsink"), (dnr,))
                        T.op("dve", lambda e: e.reciprocal(out=dn[:, g * 4:(g + 1) * 4], in_=dn[:, g * 4:(g + 1) * 4]), (dnr,), (dnr,))
                        tt("dve", o_[:, g * 4:(g + 1) * 4, :], pO3[:, :, 0:64],
                           dn[:, g * 4:(g + 1) * 4].rearrange("p (j o) -> p j o", o=1).to_broadcast([128, 4, 64]), ALU.mult,
                           (pOr, dnr), (o_r,))
                    o_, o_r = oa[t % 2]
                    pt, ptr = pb()
                    transposes([(pt[:, c * 128:(c + 1) * 128], o_[:, 2 * c:2 * c + 2, :].rearrange("p a b -> p (a b)")) for c in range(4)], (o_r,), (ptr,))
                    tt("dve", ysT[:, 0:4, t * 128:(t + 1) * 128], pt[:, 0:512].rearrange("p (c q) -> p c q", c=4),
                       gT[:, :, t * 128:(t + 1) * 128], ALU.mult, (ptr, gT_r), ("ysT",))

                pend_a = a_stage1(TL[0])
                for ti, t in enumerate(TL):
                    cur_a = pend_a
                    if ti + 1 < len(TL):
                        pend_a = a_stage1(TL[ti + 1])
                    a_stage2(t, *cur_a)

                T.barrier(("pe", "act", "dve", "sp"))
                R.reset(hT_off)
                gT, gT_r = R.alloc([128, 4, 512], BF16, "gTb")
                G = [R.alloc([128, 66, 128], BF16, f"G{k}") for k in range(3)]
                qab, qab_r = R.alloc([128, 4, 512], BF16, "qab")
                qrl, qrl_r = R.alloc([128, 2, 512], BF16, "qrl")
                Pb = [R.alloc([128, 512], BF16, f"PB{i}") for i in range(9)]
                sbf, sbf_r = R.alloc([128, 512], BF16, "sbf")
                oln, oln_r = sbf, sbf_r
                sacc, sacc_r = R.alloc([128, 512], F32, "sacc")
                sacc2, sacc2_r = R.alloc([128, 512], F32, "sacc2")
                rs_, rs_r = sacc2, sacc2_r
                xv = xoutAB.rearrange("(r p) w -> p r w", p=128)[:, :, 0:3072].rearrange("p r (t k c) -> p r t k c", t=8, k=3)
                for k in range(3):
                    for r in range(NCORES):
                        T.dma("sp", G[k][0][:, r * 8:(r + 1) * 8, :], xv[:, r, :, k, :], ("xoutAB",), (G[k][1],))
                    T.dma("sp", G[k][0][:, 64:66, :], DR["kvctx_d"][:, :, k, :], ("kvctx_d",), (G[k][1],))
                wbg, wbgres = ws.next(("gate", li, 1))
                sB = 192.0 ** -0.5
                pi = 0
                for (r0, n) in RNG:
                    for j in range(4):
                        p, pr = pf(j)
                        proj_feat(wbg, wbgres, j * 128, r0, n, p, pr)
                        act(gT[:, j, 0:n], p[:, 0:n], AF.Silu, (pr,), (gT_r,))
                    T.dma("sp", qab[:, :, 0:n], DR["qabs_d"].rearrange("h p t -> p h t")[:, :, r0:r0 + n], ("qabs_d",), (qab_r,))
                    T.dma("sp", qrl[:, :, 0:n], DR["qrT_d"][:, :, r0:r0 + n], ("qrT_d",), (qrl_r,))
                    kts = list(range(66)) if r0 < OWN else [64, 65]
                    GK = 3
                    groups = [kts[a:a + GK] for a in range(0, len(kts), GK)]
                    for h in range(4):
                        hp = h % 2
                        pOl, pOlr = psb[0][:].bitcast(F32), "psb0"
                        pMi, pMir = psb[1][:].bitcast(F32), "psb1"

                        def emit_Sg(gi_, grp):
                            base = (gi_ % 2) * GK
                            banks = [pf(base + a) for a in range(len(grp))]
                            items = []
                            for a, kt_ in enumerate(grp):
                                p = banks[a][0]
                                items.append((p[:, 0:n], G[0][0][:, kt_, :], qab[:, h, 0:n], True, False))
                                items.append((p[:, 0:n], G[1][0][hp * 64:(hp + 1) * 64, kt_, :], qrl[hp * 64:(hp + 1) * 64, h // 2, 0:n], False, True))
                            mmgroup(items, (G[0][1], G[1][1], qab_r, qrl_r), tuple(b_[1] for b_ in banks))
                            outs = []
                            for a, kt_ in enumerate(grp):
                                P_, Pr = Pb[(gi_ % 3) * GK + a]
                                act(P_[:, 0:n], banks[a][0][:, 0:n], AF.Exp, (banks[a][1],), (Pr,), scale=sB)
                                outs.append((P_, Pr))
                            return outs

                        first = [True, True]
                        LA = 1
                        pendq = [emit_Sg(g_, groups[g_]) for g_ in range(min(LA, len(groups)))]
                        for gi_, grp in enumerate(groups):
                            cur = pendq.pop(0)
                            if gi_ + LA < len(groups):
                                pendq.append(emit_Sg(gi_ + LA, groups[gi_ + LA]))
                            mmgroup([(pOl[:, 0:n], G[2][0][:, kt_, :], cur[a][0][:, 0:n], gi_ == 0 and a == 0,
                                      gi_ == len(groups) - 1 and a == len(grp) - 1) for a, kt_ in enumerate(grp)],
                                    (G[2][1],) + tuple(c_[1] for c_ in cur), (pOlr,))
                            for a in range(len(grp)):
                                w_ = 1 if (gi_ * GK + a) % 3 == 2 else 0
                                eng_ = "pool" if w_ else "dve"
                                acc_, accr_ = (sacc2, sacc2_r) if w_ else (sacc, sacc_r)
                                if first[w_]:
                                    cp(eng_, acc_[:, 0:n], cur[a][0][:, 0:n], (cur[a][1],), (accr_,))
                                    first[w_] = False
                                else:
                                    tt(eng_, acc_[:, 0:n], acc_[:, 0:n], cur[a][0][:, 0:n], ALU.add, (accr_, cur[a][1]), (accr_,))
                        if not first[1]:
                            tt("dve", sacc[:, 0:n], sacc[:, 0:n], sacc2[:, 0:n], ALU.add, (sacc_r, sacc2_r), (sacc_r,))
                        cp("act", sbf[:, 0:n], sacc[:, 0:n], (sacc_r,), (sbf_r,))
                        mmgroup([(pMi[:, 0:n], ones_bf[:], sbf[:, 0:n], True, True)], ("const", sbf_r), (pMir,))
                        T.op("dve", lambda e: e.reciprocal(out=rs_[:, 0:n], in_=pMi[:, 0:n]), (pMir,), (rs_r,))
                        tt("dve", oln[:, 0:n], pOl[:, 0:n], rs_[:, 0:n], ALU.mult, (pOlr, rs_r), (oln_r,))
                        mmgroup([(pMi[:, 0:n], wuv[:, h, :], oln[:, 0:n], True, True)], ("wuv", oln_r), (pMir,))
                        tt("dve", ysT[:, 4 + h, r0:r0 + n], pMi[:, 0:n], gT[:, h, 0:n], ALU.mult, (pMir, gT_r), ("ysT",))

                T.barrier(("pe", "act", "dve", "sp"))
                R.reset(hT_off)
                gT, gT_r = R.alloc([128, 4, TOK], BF16, "gT")
                GS = [R.alloc([128, 8, 129], F32, f"GS{i}") for i in range(2)]
                Sin, Sin_r = R.alloc([128, 128], F32, "Sin")
                Sinb, Sinb_r = R.alloc([128, 128], BF16, "Sinb")
                um, um_r = R.alloc([128, 128], F32, "um")
                am_, am_r = R.alloc([128, 1], F32, "am")
                Qhl = [R.alloc([128, OWN], BF16, f"Qhl{i}") for i in range(2)]
                osum, os_r = R.alloc([128, TOK], F32, "osum")
                sqb, sqb_r = R.alloc([128, TOK], BF16, "sqb")
                rst, rst_r = R.alloc([128, 512], F32, "rst")
                y1, y1_r = R.alloc([128, 512], F32, "y1")
                load_gate(2)
                xcv = xoutCD.rearrange("(r p) w -> p r w", p=128)
                for h in range(4):
                    T.dma("sp", osum, DR["osum_d"][h], ("osum_d",), (os_r,))
                    for d in range(2):
                        hd = d * 4 + h
                        gs, gsr = GS[hd % 2]
                        T.dma("sp", gs, xcv[:, :, hd * 129:(hd + 1) * 129], ("xoutCD",), (gsr,))
                        ql, qlr = Qhl[hd % 2]
                        T.dma("sp", ql, DR["Qh_d"][hd], ("Qh_d",), (qlr,))
                        T.dma("sp", Sin, DR["sc_d"][hd], ("sc_d",), (Sin_r,))
                        order = range(NCORES) if d == 0 else range(NCORES - 1, -1, -1)
                        so = 16 if d == 0 else 24
                        for j in order:
                            sj = sel[:, so + j:so + j + 1]
                            ts("dve", am_, gs[:, j, 0:1], -1.0, sj, ALU.add, ALU.mult, (gsr, "const"), (am_r,))
                            ts("dve", am_, am_, 1.0, None, ALU.add, None, (am_r,), (am_r,))
                            ts("dve", um, gs[:, j, 1:129], sj, None, ALU.mult, None, (gsr, "const"), (um_r,))
                            stt("dve", Sin, Sin, am_, um, ALU.mult, ALU.add, (Sin_r, am_r, um_r), (Sin_r,))
                        cp("act", Sinb, Sin, (Sin_r,), (Sinb_r,))
                        for (r0, n) in RANGES[:2]:
                            p, pr = pf()
                            mmgroup([(p[:, 0:n], Sinb, ql[:, r0:r0 + n], True, True)], (Sinb_r, qlr), (pr,))
                            tt("dve", osum[:, r0:r0 + n], osum[:, r0:r0 + n], p[:, 0:n], ALU.add, (pr, os_r), (os_r,))
                    act(sqb, osum, AF.Square, (os_r,), (sqb_r,))
                    for (r0, n) in RNG:
                        p, pr = pf()
                        mmgroup([(p[:, 0:n], ones_bf[:], sqb[:, r0:r0 + n], True, True)], ("const", sqb_r), (pr,))
                        ts("dve", rst[:, 0:n], p[:, 0:n], 1.0 / 128, EPS, ALU.mult, ALU.add, (pr,), (rst_r,))
                        T.op("act", lambda e: e.sqrt(out=rst[:, 0:n], in_=rst[:, 0:n]), (rst_r,), (rst_r,))
                        T.op("dve", lambda e: e.reciprocal(out=rst[:, 0:n], in_=rst[:, 0:n]), (rst_r,), (rst_r,))
                        stt("dve", y1[:, 0:n], osum[:, r0:r0 + n], hnormT[:, h:h + 1], rst[:, 0:n], ALU.mult, ALU.mult,
                            (os_r, "hnormT", rst_r), (y1_r,))
                        tt("dve", ysT[:, 8 + h, r0:r0 + n], y1[:, 0:n], gT[:, h, r0:r0 + n], ALU.mult, (y1_r, gT_r), ("ysT",))

                T.barrier(("pe", "act", "dve", "sp"))
                R.reset(hT_off)
                gT, gT_r = R.alloc([128, 4, TOK], BF16, "gT")
                uo, uo_r = R.alloc([128, 4, UW], F32, "uo")
                uc, uc_r = R.alloc([128, 4, UCW], F32, "uc")
                hdl, hdl_r = R.alloc([128, 8, 64], F32, "hdl")
                hL, hL_r = R.alloc([128, 32], F32, "hL")
                hR, hR_r = R.alloc([128, 32], F32, "hR")
                Ta, Ta_r = R.alloc([128, UW], F32, "Ta")
                Tb, Tb_r = R.alloc([128, UW], F32, "Tb")
                dT, dT_r = R.alloc([128, 4, TOK], BF16, "dT")
                e8, e8_r = R.alloc([128, 8], F32, "e8")
                load_gate(3)
                T.dma("sp", uo, DR["u_d"][:, :, :], ("u_d",), (uo_r,))
                T.dma("sp", uc, DR["uc_d"][:, :, :], ("uc_d",), (uc_r,))
                T.dma("sp", hdl, xcv[:, :, 8 * 129:8 * 129 + 64], ("xoutCD",), (hdl_r,))
                T.op("dve", lambda e: e.memset(hL, 0.0), (), (hL_r,))
                T.op("dve", lambda e: e.memset(hR, 0.0), (), (hR_r,))
                for j in range(NCORES):
                    stt("dve", hL, hdl[:, j, 32:64], sel[:, j:j + 1], hL, ALU.mult, ALU.add, (hdl_r, "const", hL_r), (hL_r,))
                    stt("dve", hR, hdl[:, j, 0:32], sel[:, 8 + j:9 + j], hR, ALU.mult, ALU.add, (hdl_r, "const", hR_r), (hR_r,))
                cp("dve", uo[:, :, 0:8], hL.rearrange("p (g e) -> p g e", g=4), (hL_r,), (uo_r,))
                cp("dve", uo[:, :, OWN + 8:OWN + 16], hR.rearrange("p (g e) -> p g e", g=4), (hR_r,), (uo_r,))
                for (U, U_r, W, tok0, ic) in (((uo, uo_r, UW, 0, 0),) if skipc else ((uo, uo_r, UW, 0, 0), (uc, uc_r, UCW, OWN, 1))):
                    ncen = W - 16
                    for g in range(4):
                        u = U[:, g, :]
                        tt("dve", Ta[:, 1:W], u[:, 0:W - 1], u[:, 1:W], ALU.add, (U_r, Tb_r), (Ta_r,))
                        res = Ta
                        rr = Ta_r
                        if g >= 1:
                            tt("dve", Tb[:, 2:W - 1], Ta[:, 1:W - 2], Ta[:, 3:W], ALU.add, (Ta_r,), (Tb_r,))
                            res, rr = Tb, Tb_r
                        if g >= 2:
                            tt("dve", Ta[:, 4:W - 3], Tb[:, 2:W - 5], Tb[:, 6:W - 1], ALU.add, (Tb_r,), (Ta_r,))
                            res, rr = Ta, Ta_r
                        if g >= 3:
                            tt("dve", Tb[:, 8:W - 7], Ta[:, 4:W - 11], Ta[:, 12:W - 3], ALU.add, (Ta_r,), (Tb_r,))
                            res, rr = Tb, Tb_r
                        wdt = 2 ** (g + 1)
                        stt("dve", dT[:, g, tok0:tok0 + ncen], res[:, 8:8 + ncen], 1.0 / wdt, u[:, 8:8 + ncen], ALU.mult, ALU.subtract,
                            (rr, U_r), (dT_r,))
                        for (c0, i0) in ((8, 0), (ncen, 8)):
                            tt("dve", e8, res[:, c0:c0 + 8], icnt[:, ic, g, i0:i0 + 8], ALU.mult, (rr, "const"), (e8_r,))
                            tt("dve", dT[:, g, tok0 + c0 - 8:tok0 + c0], e8, u[:, c0:c0 + 8], ALU.subtract, (e8_r, U_r), (dT_r,))
                for g in range(4):
                    for (r0, n) in RNG:
                        p, pr = pf()
                        mmgroup([(p[:, 0:n], wpool[:, g, :], dT[:, g, r0:r0 + n], True, True)], ("wpool", dT_r), (pr,))
                        stt("dve", ysT[:, 12 + g, r0:r0 + n], p[:, 0:n], pscT[:, g:g + 1], gT[:, g, r0:r0 + n], ALU.mult, ALU.mult,
                            (pr, "pscT", gT_r), ("ysT",))
                if dbg and li == 0:
                    T.dma("sp", dbg_ys[:, :, :], ysT[:], ("ysT",), ("dbg_ys",))

                T.barrier()
                R.reset(hT_off)
                mergedT, mg_r = R.alloc([128, KC, TOK], BF16, "mergedT")
                sg = [R.alloc([128, 512], F32, f"sg{i}") for i in range(3)]
                macc = [R.alloc([128, 512], F32, f"macc{i}") for i in range(2)]
                mtmp = [R.alloc([128, 512], F32, f"mtmp{i}") for i in range(2)]
                ws2 = WStream(T, [R.alloc([128, KC, 128], BF16, f"wbr{i}") for i in range(2)])
                for dc in range(16):
                    ws2.add(("wbr", li, dc), wbrblk(li, dc))
                si = 0
                mi = 0
                for dc in range(16):
                    wb, wres = ws.next(("mg", li, dc))
                    wbr_, wbrres = ws2.next(("wbr", li, dc))
                    for (r0, n) in RNG:
                        ma, mar = macc[mi % 2]
                        mi += 1
                        for nb in range(4):
                            pm, pmr = pf()
                            mmgroup([(pm[:, 0:n], wb[:, kc, nb * 128:(nb + 1) * 128], hT[:, kc, r0:r0 + n], kc == 0, kc == KC - 1) for kc in range(KC)],
                                    (wres, "hT"), (pmr,))
                            py, pyr = pf()
                            mmgroup([(py[:, 0:n], wbr_[:, nb * 4 + wc, :], ysT[:, nb * 4 + wc, r0:r0 + n], wc == 0, wc == 3) for wc in range(4)],
                                    (wbrres, "ysT"), (pyr,))
                            s_, sr = sg[si % 3]
                            si += 1
                            act(s_[:, 0:n], pm[:, 0:n], AF.Sigmoid, (pmr,), (sr,))
                            if nb == 0:
                                tt("dve", ma[:, 0:n], s_[:, 0:n], py[:, 0:n], ALU.mult, (sr, pyr), (mar,))
                            else:
                                mt, mtr = mtmp[nb % 2]
                                tt("dve", mt[:, 0:n], s_[:, 0:n], py[:, 0:n], ALU.mult, (sr, pyr), (mtr,))
                                if nb < 3:
                                    tt("pool", ma[:, 0:n], ma[:, 0:n], mt[:, 0:n], ALU.add, (mar, mtr), (mar,))
                                else:
                                    tt("pool", mergedT[:, dc, r0:r0 + n], ma[:, 0:n], mt[:, 0:n], ALU.add, (mar, mtr), (mg_r,))

                T.barrier(("pe", "act", "dve", "sp"))
                R.reset(hT_off + KC * TOK)
                ystg = [R.alloc([128, 512], F32, f"ystg{i}") for i in range(3)]
                yi = 0
                for cb in range(4):
                    wb, wres = ws.next(("wout", li, cb))
                    for t in range(NTP):
                        p, pr = pf()
                        mmgroup([(p[:, :], mergedT[:, kc, t * 128:(t + 1) * 128], wb[:, kc, :], kc == 0, kc == KC - 1) for kc in range(KC)],
                                (wres, mg_r), (pr,))
                        ys_, ysr = ystg[yi % 3]
                        yi += 1
                        cp("dve", ys_, p[:, :], (pr,), (ysr,))
                        T.dma("sp", y_d[t * 128:(t + 1) * 128, cb * 512:(cb + 1) * 512], ys_, (ysr,), ("y_d",))
                T.barrier(("pe", "act", "dve", "sp"))
                R.reset(0)
                ggl, ggl_r = R.alloc([128, 2, D], F32, "ggl")
                yb_ = [R.alloc([128, D], F32, f"yb{i}") for i in range(4)]
                xb2 = [R.alloc([128, D], F32, f"xb2{i}") for i in range(4)]
                jk, jk_r = R.alloc([128, D], BF16, "jk")
                T.dma("sp", ggl, DR["gg_d"].rearrange("s p c -> p s c"), ("gg_d",), (ggl_r,))
                def p2_load(t):
                    yb, ybr = yb_[t % 4]
                    xt, xr = xb2[t % 4]
                    T.dma("sp", yb, y_d[t * 128:(t + 1) * 128, :], ("y_d",), (ybr,))
                    if li == 0:
                        src = x_in[t * 128:(t + 1) * 128, :] if t < 8 else ctx_in[(t - 8) * 128:(t - 7) * 128, :]
                        T.dma("sp", xt, src, (), (xr,))
                    else:
                        T.dma("sp", xt, xs_r[t * 128:(t + 1) * 128, :], (xr_res,), (xr,))

                for t in range(min(2, NTP)):
                    p2_load(t)
                for t in range(NTP):
                    s = 0 if t < 8 else 1
                    if t + 2 < NTP:
                        p2_load(t + 2)
                    yb, ybr = yb_[t % 4]
                    xt, xr = xb2[t % 4]
                    ss = small[:, 48 + t:49 + t]
                    T.op("dve", lambda e: e.memset(ss, 0.0), (), (f"sso{t}",))
                    act(jk, yb, AF.Square, (ybr,), (jk_r, f"sso{t}"), accum_out=ss)
                    rstd_from_ss(ss, D, f"sso{t}")
                    stt("dve", yb, yb, ss, ggl[:, s, :], ALU.mult, ALU.mult, (ybr, f"sso{t}", ggl_r), (ybr,))
                    tt("pool", xt, xt, yb, ALU.add, (xr, ybr), (xr,))
                    if last and li == DEPTH - 1 or (last and not split):
                        if t < 8:
                            T.dma("sp", out[t * 128:(t + 1) * 128, :], xt, (xr,), ("out",))
                        if dbg:
                            T.dma("sp", dbg_xs[t * 128:(t + 1) * 128, :], xt, (xr,), ("dbg_xs",))
                    else:
                        T.dma("sp", xs_w[t * 128:(t + 1) * 128, :], xt, (xr,), (xw_res,))
            for sg_, li_ in plan:
                (pre if sg_ == 'pre' else post)(li_)
            T.barrier()
    _DECL[id(nc)] = set(declared)
    return nc


_DECL = {}


def _consts(core):
    j = np.arange(128)[:, None]
    i = np.arange(128)[None, :]
    ident = (j == i).astype(np.float32)
    mprev = (j >= i).astype(np.float32)
    mnext = (j <= i).astype(np.float32)
    maskL = mprev * (1.0 if core > 0 else 0.0)
    maskR = mnext * (1.0 if core < NCORES - 1 else 0.0)
    same = (j // CH) == (i // CH)
    Mf = (same & (j <= i)).astype(np.float32)
    Mb = (same & (j >= i)).astype(np.float32)
    chunkmask = (np.arange(128)[:, None] // CH == np.arange(4)[None, :]).astype(np.float32)
    cmask = np.concatenate([ident, mprev, mnext, maskL, maskR, Mf, chunkmask], axis=1).astype(np.float32)
    rmask = np.ones((128, TOK), np.float32)
    rmask[:, ::CH] = 0.0
    cmask2 = np.concatenate([Mb, rmask], axis=1).astype(np.float32)
    pos = core * OWN + np.arange(OWN)
    rows = (pos // 64).astype(np.float32)
    cols = (pos % 64).astype(np.float32)
    freqs = (10000.0 ** (-np.arange(16, dtype=np.float32) / 16)).astype(np.float32)
    ar = rows[:, None] * freqs[None, :]
    ac = cols[:, None] * freqs[None, :]
    C = np.concatenate([np.cos(ar), np.cos(ar), np.cos(ac), np.cos(ac)], axis=1)
    S = np.concatenate([-np.sin(ar), np.sin(ar), -np.sin(ac), np.sin(ac)], axis=1)
    ropeC = np.ascontiguousarray(C.reshape(8, 128, 64).transpose(1, 0, 2)).astype(np.float32)
    ropeS = np.ascontiguousarray(S.reshape(8, 128, 64).transpose(1, 0, 2)).astype(np.float32)
    sel = np.zeros((128, 32), np.float32)
    if core > 0:
        sel[:, core - 1] = 1.0
    if core < NCORES - 1:
        sel[:, 8 + core + 1] = 1.0
    for jj in range(NCORES):
        sel[:, 16 + jj] = 1.0 if jj < core else 0.0
        sel[:, 24 + jj] = 1.0 if jj > core else 0.0
    icnt = np.zeros((128, 2, 4, 16), np.float32)
    for which, (T_, base) in enumerate(((OWN * NCORES, core * OWN), (CTXL, 0))):
        nloc = OWN if which == 0 else CTXL
        for g, w in enumerate((2, 4, 8, 16)):
            for e in range(16):
                tl = e if e < 8 else nloc - 16 + e
                tg = base + tl
                lo = max(tg - w // 2, 0)
                hi = min(tg + w - w // 2, T_)
                icnt[:, which, g, e] = 1.0 / (hi - lo)
    return dict(cmask=cmask, cmask2=cmask2, ropeC=ropeC, ropeS=ropeS, sel=sel, icnt=icnt)


_NC_CACHE = {}
_PER_LAYER = ("w_ada", "b_ada", "g_pre", "g_post", "w_in", "mla_q_norm", "w_uq", "mla_kv_norm", "w_ukv", "w_ukT",
              "hgrn_norm", "w_pool", "pool_scale", "w_branch", "w_out")


def _prep_inputs(inputs):
    f = lambda a: np.ascontiguousarray(np.asarray(a, dtype=np.float32))
    x = f(inputs["x"])[0]
    ctx = f(inputs["ctx"])[0]
    cc = np.stack([f(inputs["c"])[0], f(inputs["c_ctx"])], axis=0)
    w_ukv = f(inputs["w_ukv"])
    w_ukT = np.ascontiguousarray(w_ukv.reshape(DEPTH, 128, 4, 2, 128)[:, :, :, 0, :].transpose(0, 2, 3, 1))
    full = dict(w_ada=f(inputs["w_ada"]), b_ada=f(inputs["b_ada"]), g_pre=f(inputs["g_pre"]), g_post=f(inputs["g_post"]),
                w_in=f(inputs["w_in"]), mla_q_norm=f(inputs["mla_q_norm"]), w_uq=f(inputs["w_uq"]),
                mla_kv_norm=f(inputs["mla_kv_norm"]), w_ukv=w_ukv, w_ukT=w_ukT, hgrn_norm=f(inputs["hgrn_norm"]),
                w_pool=f(inputs["w_pool"]), pool_scale=f(inputs["pool_scale"]), w_branch=f(inputs["w_branch"]),
                w_out=f(inputs["w_out"]))
    shared = dict(ctx=ctx, cc=cc, a_sink=f(inputs["a_sink"]).reshape(1, DEPTH * 8), hgrn_lb=f(inputs["hgrn_lb"]))
    for nm in _PER_LAYER:
        for li in range(DEPTH):
            a = full[nm][li]
            if a.ndim == 1:
                a = a.reshape(1, -1)
            shared[f"{nm}_{li}"] = a
    in_maps = []
    for c in range(NCORES):
        m = dict(shared)
        m["x"] = np.ascontiguousarray(x[c * OWN:(c + 1) * OWN])
        m.update(_consts(c))
        in_maps.append(m)
    return in_maps


def _filter(in_maps, nc):
    names = _DECL[id(nc)]
    return [{k: v for k, v in m.items() if k in names} for m in in_maps]


def kernel(**inputs):
    in_maps = _prep_inputs(inputs)
    if "nc" not in _NC_CACHE:
        _NC_CACHE["nc"] = build()
    nc = _NC_CACHE["nc"]
    res = run_bass_kernel_spmd(nc, _filter(in_maps, nc), core_ids=list(range(NCORES)))
    out = np.concatenate([np.asarray(r["out"], dtype=np.float32) for r in res.results], axis=0)
    return out.reshape(1, NCORES * OWN, D)


_SPLIT_PLANS = [[("pre", 0)], [("post", 0), ("pre", 1)], [("post", 1), ("pre", 2)], [("post", 2), ("pre", 3)], [("post", 3)]]


def kernel_unfused(**inputs):
    in_maps = _prep_inputs(inputs)
    carry = [dict() for _ in range(NCORES)]
    out = None
    for pi, plan in enumerate(_SPLIT_PLANS):
        key = ("split", pi)
        if key not in _NC_CACHE:
            _NC_CACHE[key] = build(plan=plan, split=True)
        nc = _NC_CACHE[key]
        names = _DECL[id(nc)]
        maps = []
        for c in range(NCORES):
            m = {k: v for k, v in in_maps[c].items() if k in names}
            for k in names:
                if k.startswith("r_"):
                    m[k] = carry[c][k]
            maps.append(m)
        res = run_bass_kernel_spmd(nc, maps, core_ids=list(range(NCORES)))
        rs = res.results
        if "out" in rs[0]:
            out = np.concatenate([np.asarray(r["out"], dtype=np.float32) for r in rs], axis=0)
        new = [dict() for _ in range(NCORES)]
        for c in range(NCORES):
            for k, v in rs[c].items():
                if k.startswith("w_") and k not in ("w_xinAB", "w_xinCD"):
                    new[c]["r_" + k[2:]] = np.asarray(v)
        if "w_xinAB" in rs[0]:
            gab = np.concatenate([np.asarray(rs[c]["w_xinAB"]) for c in range(NCORES)], axis=0)
            gcd = np.concatenate([np.asarray(rs[c]["w_xinCD"]) for c in range(NCORES)], axis=0)
            for c in range(NCORES):
                new[c]["r_xoutAB"] = gab
                new[c]["r_xoutCD"] = gcd
        carry = new
    return out.reshape(1, NCORES * OWN, D)
```
